# Optimizing a Trainium2 kernel written in Bass

```python
import jax, jax.numpy as jnp
from jax import lax
import numpy as np

D_MODEL = 2048
BATCH = 8
SEQ = 2048
DEPTH = 2

GRID_W = 64
CTX_LEN = 256
N_MIXERS = 2
N_POOL_LAYERS = (DEPTH + N_MIXERS - 1) // N_MIXERS
N_MLA_LAYERS = DEPTH // N_MIXERS
POOL_WIDTH = D_MODEL
POOL_GROUPS = 4
POOL_GROUP_DIM = POOL_WIDTH // POOL_GROUPS
POOL_WINDOWS = (2, 4, 8, 16)
MLA_HEADS = 16
Q_LORA_RANK = 512
KV_LORA_RANK = 512
QK_NOPE_DIM = 128
QK_ROPE_DIM = 64
V_HEAD_DIM = 128
MLA_WIDTH = MLA_HEADS * V_HEAD_DIM
MLA_IN_WIDTH = Q_LORA_RANK + KV_LORA_RANK + QK_ROPE_DIM + MLA_WIDTH
MLA_SCALE = (QK_NOPE_DIM + QK_ROPE_DIM) ** -0.5
ROPE_BASE = 10000.0
Q_BLOCK = 128
NORM_EPS = 1e-6

kernel_name = "hybrid_pool_mla_dit_prefix"


def rms_norm(x, g):
    xf = x.astype(jnp.float32)
    y = xf * lax.rsqrt(jnp.mean(xf * xf, axis=-1, keepdims=True) + NORM_EPS)
    return (y * g.astype(jnp.float32)).astype(x.dtype)


def centred_multiscale_pool(u):
    L = u.shape[1]
    uf = u.astype(jnp.float32)
    cs = jnp.concatenate([jnp.zeros_like(uf[:, :1]), jnp.cumsum(uf, axis=1)], axis=1)
    win = jnp.array(POOL_WINDOWS, dtype=jnp.int32)
    left = win // 2
    right = win - 1 - left
    t = jnp.arange(L, dtype=jnp.int32)[:, None]
    hi = jnp.minimum(t + right + 1, L)
    lo = jnp.maximum(t - left, 0)
    g = jnp.arange(POOL_GROUPS, dtype=jnp.int32)[None, :]
    s = cs[:, hi, g] - cs[:, lo, g]
    mean = s / (hi - lo).astype(jnp.float32)[None, :, :, None]
    return (mean - uf).astype(u.dtype)


def pool_branch(h, w_in, w_grp, b_grp, scale, w_out):
    B, L, _ = h.shape
    u, gate = jnp.split(h @ w_in, 2, axis=-1)
    p = centred_multiscale_pool(u.reshape(B, L, POOL_GROUPS, POOL_GROUP_DIM))
    p = jnp.einsum('blgc,gcd->blgd', p, w_grp) + b_grp
    y = p.reshape(B, L, POOL_WIDTH) * scale
    return (y * jax.nn.silu(gate)) @ w_out


def axial_rope_tables(L):
    rows = L // GRID_W
    row = jnp.repeat(jnp.arange(rows, dtype=jnp.float32), GRID_W)
    col = jnp.tile(jnp.arange(GRID_W, dtype=jnp.float32), rows)
    n = QK_ROPE_DIM // 4
    freqs = ROPE_BASE ** (-jnp.arange(n, dtype=jnp.float32) / n)
    ang = jnp.stack([row[:, None] * freqs, col[:, None] * freqs], axis=1)
    return jnp.cos(ang), jnp.sin(ang)


def apply_axial_rope(x, cos, sin):
    xs = x.reshape(x.shape[:-1] + (2, 2, QK_ROPE_DIM // 4))
    x1, x2 = xs[..., 0, :], xs[..., 1, :]
    cos = cos.astype(x.dtype)
    sin = sin.astype(x.dtype)
    out = jnp.stack([x1 * cos - x2 * sin, x1 * sin + x2 * cos], axis=-2)
    return out.reshape(x.shape)


def mla_project(h, w_in, q_norm, kv_norm, w_uq, w_ukv, rope):
    B, L, _ = h.shape
    proj = h @ w_in
    c_q, c_kv, k_r, gate = jnp.split(
        proj, [Q_LORA_RANK, Q_LORA_RANK + KV_LORA_RANK, Q_LORA_RANK + KV_LORA_RANK + QK_ROPE_DIM], axis=-1)
    q = (rms_norm(c_q, q_norm) @ w_uq).reshape(B, L, MLA_HEADS, QK_NOPE_DIM + QK_ROPE_DIM)
    q_nope, q_rope = q[..., :QK_NOPE_DIM], q[..., QK_NOPE_DIM:]
    kv = (rms_norm(c_kv, kv_norm) @ w_ukv).reshape(B, L, MLA_HEADS, QK_NOPE_DIM + V_HEAD_DIM)
    k_nope, v = kv[..., :QK_NOPE_DIM], kv[..., QK_NOPE_DIM:]
    if rope is not None:
        cos, sin = rope
        q_rope = apply_axial_rope(q_rope, cos[:, None], sin[:, None])
        k_r = apply_axial_rope(k_r, cos, sin)
    return q_nope, q_rope, k_nope, k_r, v, gate


def mla_attend(q_nope, q_rope, k_nope, k_rope, v):
    s = (jnp.einsum('bqhd,bkhd->bhqk', q_nope, k_nope)
         + jnp.einsum('bqhr,bkr->bhqk', q_rope, k_rope)).astype(jnp.float32) * MLA_SCALE
    p = jax.nn.softmax(s, axis=-1).astype(v.dtype)
    return jnp.einsum('bhqk,bkhd->bqhd', p, v)


def blocked_mla_attend(q_nope, q_rope, k_nope, k_rope, v):
    B, L, H, _ = q_nope.shape
    nb = L // Q_BLOCK
    qn = q_nope.reshape(B, nb, Q_BLOCK, H, QK_NOPE_DIM).transpose(1, 0, 2, 3, 4)
    qr = q_rope.reshape(B, nb, Q_BLOCK, H, QK_ROPE_DIM).transpose(1, 0, 2, 3, 4)
    out = lax.map(lambda qb: mla_attend(qb[0], qb[1], k_nope, k_rope, v), (qn, qr))
    return out.transpose(1, 0, 2, 3, 4).reshape(B, L, H * V_HEAD_DIM)


def mla_branch(h_lat, h_ctx, w_in, q_norm, kv_norm, w_uq, w_ukv, w_out, need_ctx_out):
    L = h_lat.shape[1]
    rope = axial_rope_tables(L)
    qn_l, qr_l, kn_l, kr_l, v_l, g_l = mla_project(h_lat, w_in, q_norm, kv_norm, w_uq, w_ukv, rope)
    qn_c, qr_c, kn_c, kr_c, v_c, g_c = mla_project(h_ctx, w_in, q_norm, kv_norm, w_uq, w_ukv, None)
    kn = jnp.concatenate([kn_c, kn_l], axis=1)
    kr = jnp.concatenate([kr_c, kr_l], axis=1)
    v = jnp.concatenate([v_c, v_l], axis=1)
    o_lat = blocked_mla_attend(qn_l, qr_l, kn, kr, v)
    y_lat = (o_lat * jax.nn.silu(g_l)) @ w_out
    if not need_ctx_out:
        return y_lat, None
    B, Lc = h_ctx.shape[:2]
    o_ctx = mla_attend(qn_c, qr_c, kn_c, kr_c, v_c).reshape(B, Lc, MLA_WIDTH)
    y_ctx = (o_ctx * jax.nn.silu(g_c)) @ w_out
    return y_lat, y_ctx


def setup_inputs(seed: int = 0) -> dict:
    key = jax.random.key(seed)
    ks = jax.random.split(key, 20)
    nrm = jax.random.normal
    f32 = jnp.float32
    return {
        "x": nrm(ks[0], (BATCH, SEQ, D_MODEL), f32),
        "c": nrm(ks[1], (BATCH, D_MODEL), f32),
        "ctx": nrm(ks[2], (BATCH, CTX_LEN, D_MODEL), f32),
        "c_ctx": nrm(ks[3], (D_MODEL,), f32),
        "ada_w": nrm(ks[4], (DEPTH, D_MODEL, 3 * D_MODEL), f32) * D_MODEL ** -0.5,
        "ada_b": nrm(ks[5], (DEPTH, 3 * D_MODEL), f32) * 0.01,
        "pre_norm": 1.0 + 0.1 * nrm(ks[6], (DEPTH, D_MODEL), f32),
        "post_norm": 1.0 + 0.1 * nrm(ks[7], (DEPTH, D_MODEL), f32),
        "pool_w_in": nrm(ks[8], (N_POOL_LAYERS, D_MODEL, 2 * POOL_WIDTH), f32) * D_MODEL ** -0.5,
        "pool_w_grp": nrm(ks[9], (N_POOL_LAYERS, POOL_GROUPS, POOL_GROUP_DIM, POOL_GROUP_DIM), f32) * POOL_GROUP_DIM ** -0.5,
        "pool_b_grp": nrm(ks[10], (N_POOL_LAYERS, POOL_GROUPS, POOL_GROUP_DIM), f32) * 0.01,
        "pool_scale": 1.0 + 0.1 * nrm(ks[11], (N_POOL_LAYERS, POOL_WIDTH), f32),
        "pool_w_out": nrm(ks[12], (N_POOL_LAYERS, POOL_WIDTH, D_MODEL), f32) * POOL_WIDTH ** -0.5,
        "mla_w_in": nrm(ks[13], (N_MLA_LAYERS, D_MODEL, MLA_IN_WIDTH), f32) * D_MODEL ** -0.5,
        "mla_q_norm": 1.0 + 0.1 * nrm(ks[14], (N_MLA_LAYERS, Q_LORA_RANK), f32),
        "mla_kv_norm": 1.0 + 0.1 * nrm(ks[15], (N_MLA_LAYERS, KV_LORA_RANK), f32),
        "mla_w_uq": nrm(ks[16], (N_MLA_LAYERS, Q_LORA_RANK, MLA_HEADS * (QK_NOPE_DIM + QK_ROPE_DIM)), f32) * Q_LORA_RANK ** -0.5,
        "mla_w_ukv": nrm(ks[17], (N_MLA_LAYERS, KV_LORA_RANK, MLA_HEADS * (QK_NOPE_DIM + V_HEAD_DIM)), f32) * KV_LORA_RANK ** -0.5,
        "mla_w_out": nrm(ks[18], (N_MLA_LAYERS, MLA_WIDTH, D_MODEL), f32) * MLA_WIDTH ** -0.5,
    }


def reference(x, c, ctx, c_ctx, ada_w, ada_b, pre_norm, post_norm,
              pool_w_in, pool_w_grp, pool_b_grp, pool_scale, pool_w_out,
              mla_w_in, mla_q_norm, mla_kv_norm, mla_w_uq, mla_w_ukv, mla_w_out):
    for i in range(DEPTH):
        last = i == DEPTH - 1
        j = i // N_MIXERS
        is_pool = (i % N_MIXERS) == 0
        need_ctx_out = not last
        need_ctx_in = need_ctx_out or not is_pool
        shift, scale, gate = jnp.split(jax.nn.silu(c) @ ada_w[i] + ada_b[i], 3, axis=-1)
        h_lat = rms_norm(x, pre_norm[i]) * (1 + scale[:, None]) + shift[:, None]
        h_ctx = None
        gate_c = None
        if need_ctx_in:
            shift_c, scale_c, gate_c = jnp.split(jax.nn.silu(c_ctx) @ ada_w[i] + ada_b[i], 3, axis=-1)
            h_ctx = rms_norm(ctx, pre_norm[i]) * (1 + scale_c) + shift_c
        if is_pool:
            y_lat = pool_branch(h_lat, pool_w_in[j], pool_w_grp[j], pool_b_grp[j], pool_scale[j], pool_w_out[j])
            y_ctx = (pool_branch(h_ctx, pool_w_in[j], pool_w_grp[j], pool_b_grp[j], pool_scale[j], pool_w_out[j])
                     if need_ctx_out else None)
        else:
            y_lat, y_ctx = mla_branch(h_lat, h_ctx, mla_w_in[j], mla_q_norm[j], mla_kv_norm[j],
                                      mla_w_uq[j], mla_w_ukv[j], mla_w_out[j], need_ctx_out)
        x = x + gate[:, None] * rms_norm(y_lat, post_norm[i])
        if need_ctx_out:
            ctx = ctx + gate_c * rms_norm(y_ctx, post_norm[i])
    return x
```

```python
import numpy as np
from contextlib import ExitStack
import concourse.bass as bass
import concourse.mybir as mybir
from concourse.bass_utils import run_bass_kernel_spmd

F32 = mybir.dt.float32
BF16 = mybir.dt.bfloat16
AF = mybir.ActivationFunctionType
ALU = mybir.AluOpType

D = 2048
KC = D // 128
LC = 256
L = 2048
NTOK = LC + L
EPS = 1e-6
POOL_WINDOWS = (2, 4, 8, 16)
H = 16
QL = 512
KVL = 512
DN = 128
DR = 64
DV = 128
MLA_IN = QL + KVL + DR + H * DV
MLA_SCALE = float((DN + DR) ** -0.5)
GRID_W = 64
DEBUG_STOP = None
DEBUG_LEVEL = 99


class Buf:
    __slots__ = ("name", "writers", "readers", "war", "excl", "last")

    def __init__(self, name="", excl=False):
        self.name = name
        self.writers = {}
        self.readers = {}
        self.war = {}
        self.excl = excl
        self.last = {}


def _merge(dst, src):
    for k, (sem, val) in src.items():
        if k not in dst or dst[k][1] < val:
            dst[k] = (sem, val)


class Q:
    def __init__(self, fw, eng, sem, dma_sems, name):
        self.fw, self.eng, self.sem, self.name = fw, eng, sem, name
        self.cnt = 0
        self.waited = {}
        self.dsems = [[s, 0] for s in dma_sems]
        self.rr = 0

    def _wait(self, deps):
        for k, (sem, val) in deps.items():
            if self.waited.get(k, 0) >= val:
                continue
            if k == self.sem.num and val > self.cnt:
                continue
            self.eng.wait_ge(sem, val)
            self.waited[k] = val

    def _deps(self, reads, writes):
        deps = {}
        for b in reads:
            if b.excl:
                _merge(deps, b.last)
                continue
            _merge(deps, b.writers)
        for b in writes:
            if b.excl:
                _merge(deps, b.last)
                continue
            if b.readers:
                b.war = b.readers
                b.readers = {}
                b.writers = {}
            _merge(deps, b.war)
        return deps

    def _record(self, tok, reads, writes):
        k = tok[0].num
        for b in list(reads) + list(writes):
            if b.excl and (k not in b.last or b.last[k][1] < tok[1]):
                b.last[k] = tok
        reads = [b for b in reads if not b.excl]
        writes = [b for b in writes if not b.excl]
        for b in reads:
            if k not in b.readers or b.readers[k][1] < tok[1]:
                b.readers[k] = tok
        for b in writes:
            if k not in b.writers or b.writers[k][1] < tok[1]:
                b.writers[k] = tok

    def op(self, fn, reads=(), writes=(), signal=True, extra=None):
        deps = self._deps(reads, writes)
        if extra:
            _merge(deps, extra)
        self._wait(deps)
        ins = fn()
        if signal:
            self.cnt += 1
            ins.then_inc(self.sem, 1)
            tok = (self.sem, self.cnt)
        else:
            tok = (self.sem, self.cnt + 1)
        self._record(tok, reads, writes)
        return tok

    def dma(self, out, in_, reads=(), writes=(), **kw):
        i = self.rr
        self.rr = (i + 1) % len(self.dsems)
        sem, cum = self.dsems[i]
        deps = self._deps(reads, writes)
        if cum:
            _merge(deps, {sem.num: (sem, cum)})
        self._wait(deps)
        self.eng.dma_start(out=out, in_=in_, **kw).then_inc(sem, 16)
        cum += 16
        self.dsems[i][1] = cum
        tok = (sem, cum)
        self._record(tok, reads, writes)
        return tok

    def all_tokens(self):
        d = {}
        if self.cnt:
            d[self.sem.num] = (self.sem, self.cnt)
        for sem, cum in self.dsems:
            if cum:
                d[sem.num] = (sem, cum)
        return d


class FW:
    def __init__(self, nc, es):
        self.nc = nc
        E = es.enter_context

        def sems(n, nm):
            return [E(nc.semaphore(f"{nm}{i}")) for i in range(n)]

        self.pe = Q(self, nc.tensor, E(nc.semaphore("s_pe")), [], "pe")
        self.act = Q(self, nc.scalar, E(nc.semaphore("s_act")), [], "act")
        self.dve = Q(self, nc.vector, E(nc.semaphore("s_dve")), [], "dve")
        self.pool = Q(self, nc.gpsimd, E(nc.semaphore("s_pool")), sems(20, "dq_pool"), "pool")
        self.sp = Q(self, nc.sync, E(nc.semaphore("s_sp")), sems(20, "dq_sp"), "sp")
        self.qs = [self.pe, self.act, self.dve, self.pool, self.sp]

    def barrier(self):
        allt = {}
        for q in self.qs:
            _merge(allt, q.all_tokens())
        for q in self.qs:
            q._wait(allt)


class Ring:
    def __init__(self, aps, name, excl=False):
        self.slots = [(ap, Buf(f"{name}{i}", excl)) for i, ap in enumerate(aps)]
        self.i = 0

    @classmethod
    def from_slots(cls, slots):
        r = cls([], "")
        r.slots = list(slots)
        return r

    def next_skip(self, live):
        for _ in range(len(self.slots)):
            s = self.next()
            if s[1] not in live:
                return s
        raise RuntimeError("no free slot")

    def next(self):
        s = self.slots[self.i]
        self.i = (self.i + 1) % len(self.slots)
        return s


def col_blocks(n, maxb=512):
    nb = (n + maxb - 1) // maxb
    base = n // nb
    rem = n - base * nb
    out = []
    s = 0
    for i in range(nb):
        sz = base + (1 if i < rem else 0)
        out.append((s, sz))
        s += sz
    return out


def build(stage):
    nc = bass.Bass("TRN2", target_bir_lowering=False)
    es = ExitStack()
    with es:
        E = es.enter_context
        fw = FW(nc, es)
        pe, act, dve, pool, sp = fw.pe, fw.act, fw.dve, fw.pool, fw.sp
        _uid = [0]

        def SB(name, shape, dt):
            _uid[0] += 1
            return nc.sbuf_tensor(f"sb_{name}_{_uid[0]}", shape, dt)

        def PS(name, shape, dt):
            _uid[0] += 1
            return nc.psum_tensor(f"ps_{name}_{_uid[0]}", shape, dt)

        def din(name, shape):
            return nc.dram_tensor(name, list(shape), F32, kind="ExternalInput").ap()

        cvec = din("cvec", (2, D))
        ada_w = din("ada_w", (2, D, 3 * D))
        ada_b = din("ada_b", (2, 3 * D))
        pre_norm = din("pre_norm", (2, D))
        post_norm = din("post_norm", (2, D))
        ident_d = din("ident", (128, 128))
        do_l0 = stage in ("l0", "fused")
        do_l1 = stage in ("l1", "fused")
        if do_l0:
            xc = din("xc", (NTOK, D))
            pool_w_in = din("pool_w_in", (D, 2 * D))
            pool_w_grp = din("pool_w_grp", (4, 512, 512))
            pool_b_grp = din("pool_b_grp", (4, 512))
            pool_scale = din("pool_scale", (D,))
            pool_w_out = din("pool_w_out", (D, D))
            inv_edge_d = din("inv_edge", (4, 16))
        if do_l1:
            mla_w_in = din("mla_w_in", (D, MLA_IN))
            mla_q_norm = din("mla_q_norm", (QL,))
            mla_kv_norm = din("mla_kv_norm", (KVL,))
            mla_w_uq = din("mla_w_uq", (QL, H * (DN + DR)))
            mla_w_ukv = din("mla_w_ukv", (KVL, H * (DN + DV)))
            mla_w_out = din("mla_w_out", (H * DV, D))
            rope_cs_d = din("rope_cs", (2, 128, L))
            rope_tok_d = din("rope_tok", (L, 2, 32))
        if stage == "l0":
            xc1 = nc.dram_tensor("xc1", [NTOK, D], F32, kind="ExternalOutput").ap()
        elif stage == "l1":
            xc1 = din("xc1", (NTOK, D))
        else:
            xc1 = nc.dram_tensor("xc1", [NTOK, D], F32).ap()
        xc1_buf = Buf("xc1")
        out_buf = Buf("out")
        if do_l1:
            outd = nc.dram_tensor("out", [L, D], F32, kind="ExternalOutput").ap()

        ident = E(SB("ident", [128, 128], BF16))
        ident_b = Buf("ident")
        pool.dma(ident[:], ident_d[:, :], writes=[ident_b])
        identf = E(SB("identf", [32, 32], F32))
        identf_b = Buf("identf")
        sp.dma(identf[:], ident_d[0:32, 0:32], writes=[identf_b])
        ones_bf = E(SB("ones_bf", [128, 128], BF16))
        ones_b = Buf("ones")
        dve.op(lambda: nc.vector.memset(ones_bf[:], 1.0), writes=[ones_b])

        pbanks = [E(PS(f"pb{i}", [128, 512], F32)) for i in range(8)]
        pall = Ring([p for p in pbanks], "pb", excl=True)
        pring6 = Ring.from_slots(pall.slots[0:6])
        pring4 = Ring.from_slots(pall.slots[0:4])
        pring = pring6

        class _PTR:
            def __getitem__(self, key):
                p, half, cols = key
                return pbanks[6 + half][:, :].bitcast(BF16)[p, cols]
        ptr = _PTR()
        ptr_bufs = [pall.slots[6][1], pall.slots[7][1]]

        rows_t = E(SB("rows_t", [32, 128], F32))
        rows_b = Buf("rows")

        def load_cols(dst, dst_buf, src_rows, n):
            sp.dma(rows_t[:n, :], src_rows, writes=[rows_b])
            pb, pb_b = pring.next()
            pe.op(lambda: nc.tensor.transpose(pb[:, 0:n], rows_t[:n, :], identf[:n, :n]), reads=[rows_b, identf_b], writes=[pb_b])
            dve.op(lambda: nc.vector.tensor_copy(out=dst, in_=pb[:, 0:n]), reads=[pb_b], writes=[dst_buf])

        A_col = E(SB("A_col", [128, KC, 2], F32))
        Sh_col = E(SB("Sh_col", [128, KC, 2], F32))
        G_box = [None]
        mod_b = Buf("mod")
        G_b = [Buf("G0"), Buf("G1")]

        def ada_layer(i, want_ctx_gate):
            with ExitStack() as s2:
                E2 = s2.enter_context
                csb = E2(SB("csb", [128, 2, KC], F32))
                sT = E2(SB("sT", [128, 2, KC], BF16))
                sT_rep = E2(SB("sT_rep", [128, 2, KC, 128], BF16))
                bcol = E2(SB("bcol", [128, 32], F32))
                pcol = E2(SB("pcol", [128, KC], F32))
                modc = E2(SB("modc", [128, 32, 2], F32))
                bg_rep = E2(SB("bg_rep", [128, D], F32))
                pn_rep = E2(SB("pn_rep", [128, D], F32))
                wblk_t = E2(SB("wblk", [128, 6, KC, 512], BF16))
                wring = Ring([wblk_t[:, j] for j in range(6)], "wblk")
                b_csb, b_sT, b_rep, b_bcol, b_pcol, b_modc, b_bg, b_pn = (Buf(n) for n in
                                                                          ("csb", "sT", "rep", "bcol", "pcol", "modc", "bg", "pn"))
                load_cols(csb[:].rearrange("p r k -> p (r k)"), b_csb, cvec.rearrange("r (kc p) -> (r kc) p", p=128), 32)
                load_cols(bcol[:], b_bcol, ada_b[i, 0:2 * D].rearrange("(c p) -> c p", p=128), 32)
                load_cols(pcol[:], b_pcol, pre_norm[i].rearrange("(c p) -> c p", p=128), 16)
                sp.dma(bg_rep[:], ada_b[i:i + 1, 2 * D:3 * D].partition_broadcast(128), writes=[b_bg])
                sp.dma(pn_rep[:], post_norm[i:i + 1, :].partition_broadcast(128), writes=[b_pn])
                act.op(lambda: nc.scalar.activation(out=sT[:], in_=csb[:], func=AF.Silu), reads=[b_csb], writes=[b_sT])
                for r in range(2):
                    dve.op(lambda r=r: nc.vector.tensor_copy(out=sT_rep[:, r], in_=sT[:, r, :].unsqueeze(2).to_broadcast([128, KC, 128])),
                           reads=[b_sT], writes=[b_rep])
                pm, pm_b = pring.next()
                pmv = pm[:, 0:64].rearrange("p (c r) -> p c r", r=2)
                for blk in range(12):
                    wb, wb_b = wring.next()
                    pool.dma(wb, ada_w[i][:, blk * 512:(blk + 1) * 512].rearrange("(kc p) n -> p kc n", p=128), writes=[wb_b])
                    if blk < 8:
                        for fc in range(4):
                            for kc in range(KC):
                                pe.op(lambda fc=fc, kc=kc, wb=wb, blk=blk: nc.tensor.matmul(
                                    pmv[:, blk * 4 + fc, :], lhsT=wb[:, kc, fc * 128:(fc + 1) * 128], rhs=sT[:, :, kc],
                                    start=(kc == 0), stop=(kc == KC - 1)),
                                    reads=[wb_b, b_sT], writes=[pm_b], signal=(kc == KC - 1))
                        if blk == 7:
                            dve.op(lambda: nc.vector.tensor_tensor(out=modc[:], in0=pmv, in1=bcol[:].unsqueeze(2).to_broadcast([128, 32, 2]),
                                                                   op=ALU.add), reads=[pm_b, b_bcol], writes=[b_modc])
                            dve.op(lambda: nc.vector.tensor_copy(out=Sh_col[:], in_=modc[:, 0:KC, :]), reads=[b_modc], writes=[mod_b])
                            dve.op(lambda: nc.vector.tensor_scalar(out=modc[:, KC:2 * KC, :], in0=modc[:, KC:2 * KC, :], scalar1=1.0,
                                                                   scalar2=None, op0=ALU.add), reads=[b_modc], writes=[b_modc])
                            dve.op(lambda: nc.vector.tensor_tensor(out=A_col[:], in0=modc[:, KC:2 * KC, :],
                                                                   in1=pcol[:].unsqueeze(2).to_broadcast([128, KC, 2]), op=ALU.mult),
                                   reads=[b_modc, b_pcol], writes=[mod_b])
                    else:
                        c0 = (blk - 8) * 512
                        for r in range(2 if want_ctx_gate else 1):
                            pg, pg_b = pring.next()
                            for kc in range(KC):
                                pe.op(lambda r=r, kc=kc, wb=wb, pg=pg: nc.tensor.matmul(
                                    pg[:], lhsT=sT_rep[:, r, kc, :], rhs=wb[:, kc, :], start=(kc == 0), stop=(kc == KC - 1)),
                                    reads=[wb_b, b_rep], writes=[pg_b], signal=(kc == KC - 1))
                            dve.op(lambda r=r, pg=pg, c0=c0: nc.vector.tensor_tensor(out=G_box[0][:, r, c0:c0 + 512], in0=pg[:],
                                                                                 in1=bg_rep[:, c0:c0 + 512], op=ALU.add),
                                   reads=[pg_b, b_bg], writes=[G_b[r]])
                            dve.op(lambda r=r, c0=c0: nc.vector.tensor_tensor(out=G_box[0][:, r, c0:c0 + 512], in0=G_box[0][:, r, c0:c0 + 512],
                                                                          in1=pn_rep[:, c0:c0 + 512], op=ALU.mult),
                                   reads=[G_b[r], b_pn], writes=[G_b[r]])
                fw.barrier()

        def norm_tile_A(src_ap, src_buf, nrows, xt_slot, xn_slot, small):
            xt, xt_b = xt_slot
            xn, xn_b = xn_slot
            (ss, rstd, _), sm_b = small
            sp.dma(xt[:nrows, :], src_ap, reads=[src_buf], writes=[xt_b])
            act.op(lambda: nc.scalar.activation(out=xn[:nrows, :], in_=xt[:nrows, :], func=AF.Square, accum_out=ss[:nrows, :]),
                   reads=[xt_b], writes=[sm_b, xn_b])
            act.op(lambda: nc.scalar.activation(out=rstd[:nrows, :], in_=ss[:nrows, :], func=AF.Sqrt, scale=1.0 / D, bias=eps_col[:nrows, :]),
                   reads=[sm_b, eps_b], writes=[sm_b])
            dve.op(lambda: nc.vector.reciprocal(out=rstd[:nrows, :], in_=rstd[:nrows, :]), reads=[sm_b], writes=[sm_b])
            dve.op(lambda: nc.vector.tensor_scalar(out=xn[:nrows, :], in0=xt[:nrows, :], scalar1=rstd[:nrows, :], scalar2=None, op0=ALU.mult),
                   reads=[xt_b, sm_b], writes=[xn_b])
            return (xn, xn_b, nrows)

        def norm_tile_B(state, seg, hT_dst, hT_buf):
            xn, xn_b, nrows = state
            for half in range(2):
                for j in range(8):
                    fc = half * 8 + j
                    pe.op(lambda fc=fc, half=half, j=j: nc.tensor.transpose(ptr[:, half, j * 128:j * 128 + nrows],
                                                                          xn[:nrows, fc * 128:(fc + 1) * 128], ident[:nrows, :nrows]),
                          reads=[xn_b, ident_b], writes=[ptr_bufs[half]], signal=(j == 7))
                for j in range(8):
                    fc = half * 8 + j
                    if half == 0:
                        act.op(lambda fc=fc, half=half, j=j: nc.scalar.activation(
                            out=hT_dst(fc), in_=ptr[:, half, j * 128:j * 128 + nrows], func=AF.Identity,
                            scale=A_col[:, fc, seg:seg + 1], bias=Sh_col[:, fc, seg:seg + 1]),
                            reads=[ptr_bufs[half], mod_b], writes=[hT_buf])
                    else:
                        dve.op(lambda fc=fc, half=half, j=j: nc.vector.tensor_scalar(
                            out=hT_dst(fc), in0=ptr[:, half, j * 128:j * 128 + nrows], scalar1=A_col[:, fc, seg:seg + 1],
                            scalar2=Sh_col[:, fc, seg:seg + 1], op0=ALU.mult, op1=ALU.add),
                            reads=[ptr_bufs[half], mod_b], writes=[hT_buf])

        def norm_tile_to_hT(src_ap, src_buf, nrows, seg, xt_slot, xn_slot, small, hT_dst, hT_buf, junk):
            norm_tile_B(norm_tile_A(src_ap, src_buf, nrows, xt_slot, xn_slot, small), seg, hT_dst, hT_buf)

        def out_proj_tile(lhsT_fn, lhs_buf, wout, b_wout, seg, x_src, x_src_buf, dst, dst_buf, big, smalls, junk):
            lhs_bufs = lhs_buf if isinstance(lhs_buf, list) else [lhs_buf] * KC
            xt, xt_b = big.next()
            ot, ot_b = big.next()
            sp.dma(xt[:], x_src, reads=[x_src_buf], writes=[xt_b])
            (ss, rstd, ss4), sm_b = smalls.next()
            banks = []
            for n in range(4):
                pb, pb_b = pring.next()
                banks.append((pb, pb_b))
                for kc in range(KC):
                    pe.op(lambda kc=kc, pb=pb, n=n: nc.tensor.matmul(
                        pb[:], lhsT=lhsT_fn(kc), rhs=wout[:, kc, n * 512:(n + 1) * 512],
                        start=(kc == 0), stop=(kc == KC - 1)),
                        reads=[lhs_bufs[kc], b_wout], writes=[pb_b], signal=(kc == KC - 1))
                act.op(lambda pb=pb, n=n, ss4=ss4: nc.scalar.activation(
                    out=ot[:, n * 512:(n + 1) * 512], in_=pb[:], func=AF.Square, accum_out=ss4[:, n:n + 1]),
                    reads=[pb_b], writes=[sm_b, ot_b])
                dve.op(lambda pb=pb, n=n: nc.vector.tensor_tensor(
                    out=ot[:, n * 512:(n + 1) * 512], in0=pb[:], in1=G_box[0][:, seg, n * 512:(n + 1) * 512], op=ALU.mult),
                    reads=[pb_b, G_b[seg]], writes=[ot_b])
            dve.op(lambda: nc.vector.tensor_reduce(out=ss, in_=ss4, axis=mybir.AxisListType.X, op=ALU.add),
                   reads=[sm_b], writes=[sm_b])
            act.op(lambda: nc.scalar.activation(out=rstd, in_=ss, func=AF.Sqrt, scale=1.0 / D, bias=eps_col[:]),
                   reads=[sm_b, eps_b], writes=[sm_b])
            dve.op(lambda: nc.vector.reciprocal(out=rstd, in_=rstd), reads=[sm_b], writes=[sm_b])
            dve.op(lambda: nc.vector.scalar_tensor_tensor(out=ot[:], in0=ot[:], scalar=rstd, in1=xt[:], op0=ALU.mult, op1=ALU.add),
                   reads=[xt_b, ot_b, sm_b], writes=[ot_b])
            sp.dma(dst, ot[:], reads=[ot_b], writes=[dst_buf])

        eps_col = E(SB("eps_col", [128, 1], F32))
        eps_b = Buf("eps")
        dve.op(lambda: nc.vector.memset(eps_col[:], EPS), writes=[eps_b])

        def layer0():
            with SB("G_rows0", [128, 2, D], F32) as g_:
                G_box[0] = g_
                layer0_body()

        def layer0_body():
            ada_layer(0, True)
            if DEBUG_STOP == "ada":
                return
            def lat(o0, o1):
                c0 = max(o0 - 8, 0)
                c1 = min(o1 + 8, L)
                return dict(seg=0, row0=LC + c0, ncomp=c1 - c0, olo=o0 - c0, ohi=o1 - c0, ledge=(o0 == 0), redge=(o1 == L))
            ctxseg = dict(seg=1, row0=0, ncomp=LC, olo=0, ohi=LC, ledge=True, redge=True)
            supers = [[ctxseg, lat(0, 384)], [lat(384, 896)], [lat(896, 1408)], [lat(1408, 2048)]]
            MAXC = 656
            with ExitStack() as s2:
                E2 = s2.enter_context
                hT = E2(SB("hT", [128, KC, MAXC], BF16))
                zT = E2(SB("zT", [128, KC, 640], BF16))
                wout = E2(SB("wout", [128, KC, D], BF16))
                wg = E2(SB("wg", [128, 4, 4, 512], BF16))
                ubuf_t = E2(SB("ubuf", [128, 2, MAXC + 32], F32))
                tmp_t = E2(SB("ptmp", [128, 2, MAXC + 32], F32))
                pT = E2(SB("pT", [128, 4, 640], BF16))
                sgT_t = E2(SB("sgT", [128, 2, MAXC], BF16))
                t1_t = E2(SB("t1", [128, 2, 640], BF16))
                wch_t = E2(SB("wch", [128, 3, KC, 128], BF16))
                big_t = E2(SB("big", [128, 3, D], F32))
                xn_t = E2(SB("xn", [128, 2, D], BF16))
                small_t = E2(SB("small", [128, 4, 8], F32))
                bgcol = E2(SB("bgcol", [128, 16], F32))
                pscol = E2(SB("pscol", [128, 16], F32))
                bscol = E2(SB("bscol", [128, 16], F32))
                inv_edge = E2(SB("inv_edge", [128, 4, 16], F32))
                edge_tmp = E2(SB("edge_tmp", [128, 16], F32))
                b_wout, b_wg, b_cols, b_inv, b_edge = Buf("wout"), Buf("wg"), Buf("cols"), Buf("inv"), Buf("edge")
                b_hT, b_zT, b_pT = Buf("hT"), Buf("zT"), [Buf(f"pT{c}") for c in range(4)]
                ubuf = Ring([ubuf_t[:, j] for j in range(2)], "ubuf")
                tmpb = [Buf("tmp0"), Buf("tmp1")]
                sgT = Ring([sgT_t[:, j] for j in range(2)], "sgT")
                t1r = Ring([t1_t[:, j] for j in range(2)], "t1")
                wch = Ring([wch_t[:, j] for j in range(3)], "wch")
                big = Ring([big_t[:, j] for j in range(3)], "big")
                xnr = Ring([xn_t[:, j] for j in range(2)], "xn")
                smalls = Ring([(small_t[:, j, 0:1], small_t[:, j, 1:2], small_t[:, j, 4:8]) for j in range(4)], "small")
                junk = None

                for j in range(2):
                    dve.op(lambda j=j: nc.vector.memset(ubuf_t[:, j], 0.0), writes=[ubuf.slots[j][1]])
                for g in range(4):
                    pool.dma(wg[:, g], pool_w_grp[g].rearrange("(c p) d -> p c d", p=128), writes=[b_wg])
                load_cols(bgcol[:], b_cols, pool_b_grp.rearrange("g (c p) -> (g c) p", p=128), 16)
                load_cols(pscol[:], b_cols, pool_scale.rearrange("(c p) -> c p", p=128), 16)
                sp.dma(inv_edge[:], inv_edge_d.rearrange("g e -> (g e)").partition_broadcast(128), writes=[b_inv])
                dve.op(lambda: nc.vector.tensor_tensor(out=bscol[:], in0=bgcol[:], in1=pscol[:], op=ALU.mult), reads=[b_cols], writes=[b_cols])
                wout_todo = list(range(4))

                def load_wout_panel():
                    if wout_todo:
                        n = wout_todo.pop(0)
                        pool.dma(wout[:, :, n * 512:(n + 1) * 512], pool_w_out[:, n * 512:(n + 1) * 512].rearrange("(kc p) n -> p kc n", p=128),
                                 writes=[b_wout])

                for st in supers:
                    c = 0
                    oc = 0
                    for sg in st:
                        sg["c0"] = c
                        sg["u0"] = c + 8 + 16 * st.index(sg)
                        sg["o0"] = oc
                        c += sg["ncomp"]
                        oc += sg["ohi"] - sg["olo"]

                def p1_tiles(st):
                    out = []
                    for sg in st:
                        r = 0
                        while r < sg["ncomp"]:
                            n = min(128, sg["ncomp"] - r)
                            out.append((sg, r, n))
                            r += n
                    return out

                def p1_A(tl):
                    sg, r, n = tl
                    return norm_tile_A(xc[sg["row0"] + r: sg["row0"] + r + n, :], Buf("xcin"), n, big.next(), xnr.next(), smalls.next())

                def p1_B(tl, stt):
                    sg, r, n = tl
                    col = sg["c0"] + r
                    norm_tile_B(stt, sg["seg"], lambda fc: hT[:, fc, col:col + n], b_hT)

                def p1_run(tiles):
                    if not tiles:
                        return
                    stt = p1_A(tiles[0])
                    for j, tl in enumerate(tiles):
                        nxt = p1_A(tiles[j + 1]) if j + 1 < len(tiles) else None
                        p1_B(tl, stt)
                        stt = nxt

                p1_run(p1_tiles(supers[0]))
                for si, st in enumerate(supers):
                    ncomp = sum(sg["ncomp"] for sg in st)
                    nout = sum(sg["ohi"] - sg["olo"] for sg in st)
                    for (ub_, ubb_) in ubuf.slots:
                        for sg in st:
                            if sg["ledge"]:
                                dve.op(lambda ub_=ub_, sg=sg: nc.vector.memset(ub_[:, sg["u0"] - 8: sg["u0"]], 0.0), writes=[ubb_])
                            if sg["redge"]:
                                dve.op(lambda ub_=ub_, sg=sg: nc.vector.memset(ub_[:, sg["u0"] + sg["ncomp"]: sg["u0"] + sg["ncomp"] + 8], 0.0),
                                       writes=[ubb_])
                    cblocks = []
                    for sg in st:
                        for (s, sz) in col_blocks(sg["ncomp"]):
                            cblocks.append((sg, s, sz))
                    oblocks = col_blocks(nout)
                    for g in range(4):
                        w = POOL_WINDOWS[g]
                        for cc in range(4):
                            j = 4 * g + cc
                            wc, wc_b = wch.next()
                            pool.dma(wc, pool_w_in[:, j * 128:(j + 1) * 128].rearrange("(kc p) f -> p kc f", p=128), writes=[wc_b])
                            ub, ub_b = ubuf.next()
                            for (sg, s, sz) in cblocks:
                                pb, pb_b = pring.next()
                                for kc in range(KC):
                                    pe.op(lambda kc=kc, wc=wc, pb=pb, sg=sg, s=s, sz=sz: nc.tensor.matmul(
                                        pb[:, 0:sz], lhsT=wc[:, kc, :], rhs=hT[:, kc, sg["c0"] + s: sg["c0"] + s + sz],
                                        start=(kc == 0), stop=(kc == KC - 1)),
                                        reads=[wc_b, b_hT], writes=[pb_b], signal=(kc == KC - 1))
                                act.op(lambda pb=pb, ub=ub, sg=sg, s=s, sz=sz: nc.scalar.copy(out=ub[:, sg["u0"] + s: sg["u0"] + s + sz], in_=pb[:, 0:sz]),
                                       reads=[pb_b], writes=[ub_b])
                            for sg in st:
                                NP = sg["ncomp"] + 16
                                b0 = sg["u0"] - 8
                                U = lambda a, b, ub=ub, b0=b0: ub[:, b0 + a: b0 + b]
                                T0 = lambda a, b, b0=b0: tmp_t[:, 0, b0 + a: b0 + b]
                                T1 = lambda a, b, b0=b0: tmp_t[:, 1, b0 + a: b0 + b]
                                dve.op(lambda U=U, T0=T0, NP=NP: nc.vector.tensor_tensor(out=T0(1, NP), in0=U(0, NP - 1), in1=U(1, NP), op=ALU.add),
                                       reads=[ub_b], writes=[tmpb[0]])
                                S, Sb = T0, tmpb[0]
                                if w >= 4:
                                    dve.op(lambda T0=T0, T1=T1, NP=NP: nc.vector.tensor_tensor(out=T1(2, NP - 1), in0=T0(3, NP), in1=T0(1, NP - 2), op=ALU.add),
                                           reads=[tmpb[0]], writes=[tmpb[1]])
                                    S, Sb = T1, tmpb[1]
                                if w >= 8:
                                    dve.op(lambda T0=T0, T1=T1, NP=NP: nc.vector.tensor_tensor(out=T0(4, NP - 3), in0=T1(6, NP - 1), in1=T1(2, NP - 5), op=ALU.add),
                                           reads=[tmpb[1]], writes=[tmpb[0]])
                                    S, Sb = T0, tmpb[0]
                                if w >= 16:
                                    dve.op(lambda T0=T0, T1=T1, NP=NP: nc.vector.tensor_tensor(out=T1(8, NP - 7), in0=T0(12, NP - 3), in1=T0(4, NP - 11), op=ALU.add),
                                           reads=[tmpb[0]], writes=[tmpb[1]])
                                    S, Sb = T1, tmpb[1]
                                lo, hi = 8 + sg["olo"], 8 + sg["ohi"]
                                no = hi - lo
                                o0 = sg["o0"]
                                dve.op(lambda S=S, U=U, lo=lo, hi=hi, o0=o0, no=no, cc=cc, w=w: nc.vector.scalar_tensor_tensor(
                                    out=pT[:, cc, o0:o0 + no], in0=S(lo, hi), scalar=1.0 / w, in1=U(lo, hi), op0=ALU.mult, op1=ALU.subtract),
                                    reads=[Sb, ub_b], writes=[b_pT[cc]])
                                for (is_edge, elo, tcol, ocol) in ((sg["ledge"], lo, 0, o0), (sg["redge"], hi - 8, 8, o0 + no - 8)):
                                    if not is_edge:
                                        continue
                                    dve.op(lambda S=S, elo=elo, tcol=tcol, g=g: nc.vector.tensor_tensor(
                                        out=edge_tmp[:, 0:8], in0=S(elo, elo + 8), in1=inv_edge[:, g, tcol:tcol + 8], op=ALU.mult),
                                        reads=[Sb, b_inv], writes=[b_edge])
                                    dve.op(lambda U=U, elo=elo, ocol=ocol, cc=cc: nc.vector.tensor_tensor(
                                        out=pT[:, cc, ocol:ocol + 8], in0=edge_tmp[:, 0:8], in1=U(elo, elo + 8), op=ALU.subtract),
                                        reads=[b_edge, ub_b], writes=[b_pT[cc]])
                        for dd in range(4):
                            jj = 16 + 4 * g + dd
                            zc = 4 * g + dd
                            wc, wc_b = wch.next()
                            pool.dma(wc, pool_w_in[:, jj * 128:(jj + 1) * 128].rearrange("(kc p) f -> p kc f", p=128), writes=[wc_b])
                            if dd == 3:
                                load_wout_panel()
                            sgt, sgt_b = sgT.next()
                            for sg in st:
                                olo, ohi = sg["olo"], sg["ohi"]
                                for (s, sz) in col_blocks(ohi - olo):
                                    pb, pb_b = pring.next()
                                    for kc in range(KC):
                                        pe.op(lambda kc=kc, wc=wc, pb=pb, sg=sg, s=s, sz=sz, olo=olo: nc.tensor.matmul(
                                            pb[:, 0:sz], lhsT=wc[:, kc, :], rhs=hT[:, kc, sg["c0"] + olo + s: sg["c0"] + olo + s + sz],
                                            start=(kc == 0), stop=(kc == KC - 1)),
                                            reads=[wc_b, b_hT], writes=[pb_b], signal=(kc == KC - 1))
                                    act.op(lambda pb=pb, sgt=sgt, sg=sg, s=s, sz=sz: nc.scalar.activation(
                                        out=sgt[:, sg["o0"] + s: sg["o0"] + s + sz], in_=pb[:, 0:sz], func=AF.Silu),
                                        reads=[pb_b], writes=[sgt_b])
                            t1, t1_b = t1r.next()
                            for (s, sz) in oblocks:
                                pb, pb_b = pring.next()
                                for cc in range(4):
                                    pe.op(lambda cc=cc, pb=pb, s=s, sz=sz, dd=dd, g=g: nc.tensor.matmul(
                                        pb[:, 0:sz], lhsT=wg[:, g, cc, dd * 128:(dd + 1) * 128], rhs=pT[:, cc, s:s + sz],
                                        start=(cc == 0), stop=(cc == 3)),
                                        reads=[b_wg, b_pT[cc]], writes=[pb_b], signal=(cc == 3))
                                act.op(lambda pb=pb, t1=t1, s=s, sz=sz, zc=zc: nc.scalar.activation(
                                    out=t1[:, s:s + sz], in_=pb[:, 0:sz], func=AF.Identity, scale=pscol[:, zc:zc + 1], bias=bscol[:, zc:zc + 1]),
                                    reads=[pb_b, b_cols], writes=[t1_b])
                            dve.op(lambda t1=t1, sgt=sgt, zc=zc, nout=nout: nc.vector.tensor_tensor(
                                out=zT[:, zc, 0:nout], in0=t1[:, 0:nout], in1=sgt[:, 0:nout], op=ALU.mult),
                                reads=[t1_b, sgt_b], writes=[b_zT])
                    if DEBUG_STOP == "p2":
                        break
                    nxt_tiles = p1_tiles(supers[si + 1]) if si + 1 < len(supers) else []
                    for sg in st:
                        ntile = (sg["ohi"] - sg["olo"]) // 128
                        for tt in range(ntile):
                            zc0 = sg["o0"] + tt * 128
                            row = sg["row0"] + sg["olo"] + tt * 128
                            tl = nxt_tiles.pop(0) if nxt_tiles else None
                            stt = p1_A(tl) if tl is not None else None
                            out_proj_tile(lambda kc, zc0=zc0: zT[:, kc, zc0:zc0 + 128], b_zT, wout, b_wout, sg["seg"],
                                          xc[row:row + 128, :], Buf("xin"), xc1[row:row + 128, :], xc1_buf, big, smalls, junk)
                            if tl is not None:
                                p1_B(tl, stt)
                    p1_run(nxt_tiles)
                fw.barrier()

        def layer1():
            with SB("G_rows1", [128, 1, D], F32) as g_:
                G_box[0] = g_
                layer1_body()

        def layer1_body():
            ada_layer(1, False)
            NKT = NTOK // 128
            with ExitStack() as s1:
                E1 = s1.enter_context
                oT = E1(SB("oT", [128, H, L], BF16))
                b_oT = [Buf(f"oT{h}") for h in range(H)]
                small_t = E1(SB("small1", [128, 8, 8], F32))
                smalls = Ring([(small_t[:, j, 0:1], small_t[:, j, 1:2], small_t[:, j, 4:8]) for j in range(8)], "small1")
                with ExitStack() as sab:
                    EAB = sab.enter_context
                    ckvnT = EAB(SB("ckvnT", [128, 4, NTOK], BF16))
                    kr2T = EAB(SB("kr2T", [128, NTOK], BF16))
                    cqnT = EAB(SB("cqnT", [128, 4, L], BF16))
                    b_ckvn, b_kr2, b_cqn = Buf("ckvnT"), Buf("kr2T"), Buf("cqnT")
                    qn_col = EAB(SB("qn_col", [128, 4], F32))
                    kvn_col = EAB(SB("kvn_col", [128, 4], F32))
                    b_ncol = Buf("ncol")
                    load_cols(qn_col[:], b_ncol, mla_q_norm.rearrange("(c p) -> c p", p=128), 4)
                    load_cols(kvn_col[:], b_ncol, mla_kv_norm.rearrange("(c p) -> c p", p=128), 4)
                    with ExitStack() as sa:
                        EA = sa.enter_context
                        junk = EA(SB("junkA", [128, 512], BF16))
                        WA = EA(SB("WA", [128, KC, QL + KVL + DR], BF16))
                        b_WA = Buf("WA")
                        for (c0, c1) in ((QL, QL + KVL + DR), (0, QL)):
                            pool.dma(WA[:, :, c0:c1], mla_w_in[:, c0:c1].rearrange("(kc p) n -> p kc n", p=128), writes=[b_WA])
                        hA_t = EA(SB("hA", [128, 4, KC, 128], BF16))
                        hAr = Ring([hA_t[:, j] for j in range(4)], "hA")
                        big_t = EA(SB("bigA", [128, 3, D], F32))
                        big = Ring([big_t[:, j] for j in range(3)], "bigA")
                        xn_t = EA(SB("xnA", [128, 2, D], BF16))
                        xnr = Ring([xn_t[:, j] for j in range(2)], "xnA")
                        cn_t = EA(SB("cn", [128, 6, 512], BF16))
                        cnr = Ring([cn_t[:, j] for j in range(6)], "cn")
                        kr_t = EA(SB("kr", [128, 2, 64], F32))
                        krr = Ring([kr_t[:, j] for j in range(2)], "kr")
                        kro_t = EA(SB("kro", [128, 4, 128], BF16))
                        kror = Ring([kro_t[:, j] for j in range(4)], "kro")
                        rt_t = EA(SB("rt", [128, 2, 2, 32], F32))
                        rtr = Ring([rt_t[:, j] for j in range(2)], "rt")
                        tt_t = EA(SB("ropetmp", [128, 4, 32], F32))
                        b_tt = Buf("ropetmp")

                        def rms512_A(pb, pb_b, small):
                            (ss, rstd, _), sm_b = small
                            cn, cn_b = cnr.next()
                            act.op(lambda: nc.scalar.activation(out=junk[:, 0:512], in_=pb[:, 0:512], func=AF.Square, accum_out=ss),
                                   reads=[pb_b], writes=[sm_b])
                            act.op(lambda: nc.scalar.activation(out=rstd, in_=ss, func=AF.Sqrt, scale=1.0 / 512, bias=eps_col[:]),
                                   reads=[sm_b, eps_b], writes=[sm_b])
                            dve.op(lambda: nc.vector.reciprocal(out=rstd, in_=rstd), reads=[sm_b], writes=[sm_b])
                            dve.op(lambda: nc.vector.tensor_scalar(out=cn[:], in0=pb[:, 0:512], scalar1=rstd, scalar2=None, op0=ALU.mult),
                                   reads=[pb_b, sm_b], writes=[cn_b])
                            return cn, cn_b

                        def rms512_B(cnst, gcol, dst, dst_b):
                            cn, cn_b = cnst
                            for c in range(4):
                                pe.op(lambda c=c: nc.tensor.transpose(ptr[:, 1, c * 128:(c + 1) * 128], cn[:, c * 128:(c + 1) * 128], ident[:]),
                                      reads=[cn_b, ident_b], writes=[ptr_bufs[1]], signal=(c == 3))
                            dve.op(lambda: nc.vector.tensor_tensor(out=dst, in0=ptr[:, 1, 0:512].rearrange("p (c t) -> p c t", c=4),
                                                                   in1=gcol[:].unsqueeze(2).to_broadcast([128, 4, 128]), op=ALU.mult),
                                   reads=[ptr_bufs[1], b_ncol], writes=[dst_b])

                        st_ = {}

                        def stA_N(ti):
                            row = ti * 128
                            seg = 0 if ti >= LC // 128 else 1
                            hA, hA_b = hAr.next()
                            norm_tile_to_hT(xc1[row:row + 128, :], xc1_buf, 128, seg, big.next(), xnr.next(), smalls.next(),
                                            lambda fc, hA=hA: hA[:, fc, :], hA_b, junk)
                            st_[ti] = dict(hA=hA, hA_b=hA_b)

                        def stA_P(ti):
                            is_lat = ti >= LC // 128
                            hA, hA_b = st_[ti]["hA"], st_[ti]["hA_b"]
                            pkv, pkv_b = pring.next()
                            pkr, pkr_b = pring.next()
                            outs = [(pkv, pkv_b, QL, KVL), (pkr, pkr_b, QL + KVL, DR)]
                            if is_lat:
                                pq, pq_b = pring.next()
                                outs.append((pq, pq_b, 0, QL))
                                st_[ti].update(pq=pq, pq_b=pq_b)
                            for (pb, pb_b, c0, w) in outs:
                                for kc in range(KC):
                                    pe.op(lambda kc=kc, pb=pb, c0=c0, w=w, hA=hA: nc.tensor.matmul(
                                        pb[:, 0:w], lhsT=hA[:, kc, :], rhs=WA[:, kc, c0:c0 + w], start=(kc == 0), stop=(kc == KC - 1)),
                                        reads=[hA_b, b_WA], writes=[pb_b], signal=(kc == KC - 1))
                            st_[ti].update(pkv=pkv, pkv_b=pkv_b, pkr=pkr, pkr_b=pkr_b)

                        rst_ = {}

                        def stA_Ra(ti):
                            row = ti * 128
                            is_lat = ti >= LC // 128
                            S_ = st_.pop(ti)
                            pkv, pkv_b, pkr, pkr_b = S_["pkv"], S_["pkv_b"], S_["pkr"], S_["pkr_b"]
                            R_ = dict(kv=rms512_A(pkv, pkv_b, smalls.next()))
                            if is_lat:
                                t0 = row - LC
                                R_["q"] = rms512_A(S_["pq"], S_["pq_b"], smalls.next())
                            kr, kr_b = krr.next()
                            kro, kro_b = kror.next()
                            dve.op(lambda kr=kr, pkr=pkr: nc.vector.tensor_copy(out=kr, in_=pkr[:, 0:64]), reads=[pkr_b], writes=[kr_b])
                            if is_lat:
                                rt, rt_b = rtr.next()
                                sp.dma(rt, rope_tok_d[t0:t0 + 128], writes=[rt_b])
                                krv = kr.rearrange("p (h j n) -> p h j n", h=2, j=2)
                                x1, x2 = krv[:, :, 0, :], krv[:, :, 1, :]
                                cosv = rt[:, 0, :].rearrange("p (h n) -> p h n", h=2)
                                sinv = rt[:, 1, :].rearrange("p (h n) -> p h n", h=2)
                                tv = [tt_t[:, j, :].rearrange("p (h n) -> p h n", h=2) for j in range(4)]
                                krov = kro[:, 0:64].rearrange("p (h j n) -> p h j n", h=2, j=2)
                                for (o, a, bb) in ((tv[0], x1, cosv), (tv[1], x2, sinv), (tv[2], x1, sinv), (tv[3], x2, cosv)):
                                    dve.op(lambda o=o, a=a, bb=bb: nc.vector.tensor_tensor(out=o, in0=a, in1=bb, op=ALU.mult),
                                           reads=[kr_b, rt_b], writes=[b_tt])
                                dve.op(lambda: nc.vector.tensor_tensor(out=krov[:, :, 0, :], in0=tv[0], in1=tv[1], op=ALU.subtract),
                                       reads=[b_tt], writes=[kro_b])
                                dve.op(lambda: nc.vector.tensor_tensor(out=krov[:, :, 1, :], in0=tv[2], in1=tv[3], op=ALU.add),
                                       reads=[b_tt], writes=[kro_b])
                            else:
                                dve.op(lambda kr=kr, kro=kro: nc.vector.tensor_copy(out=kro[:, 0:64], in_=kr), reads=[kr_b], writes=[kro_b])
                            dve.op(lambda kro=kro: nc.vector.tensor_copy(out=kro[:, 64:128], in_=kro[:, 0:64]), reads=[kro_b], writes=[kro_b])
                            R_["kro"] = (kro, kro_b)
                            rst_[ti] = R_

                        def stA_Rb(ti):
                            row = ti * 128
                            R_ = rst_.pop(ti)
                            rms512_B(R_["kv"], kvn_col, ckvnT[:, :, row:row + 128], b_ckvn)
                            if "q" in R_:
                                t0 = row - LC
                                rms512_B(R_["q"], qn_col, cqnT[:, :, t0:t0 + 128], b_cqn)
                            kro, kro_b = R_["kro"]
                            pe.op(lambda kro=kro: nc.tensor.transpose(ptr[:, 0, 0:128], kro, ident[:]), reads=[kro_b, ident_b], writes=[ptr_bufs[0]])
                            act.op(lambda row=row: nc.scalar.copy(out=kr2T[:, row:row + 128], in_=ptr[:, 0, 0:128]), reads=[ptr_bufs[0]], writes=[b_kr2])

                        SKEW = 2
                        for ti in range(min(SKEW, NKT)):
                            stA_N(ti)
                        stA_P(0)
                        for ti in range(NKT):
                            if ti + SKEW < NKT:
                                stA_N(ti + SKEW)
                            if ti + 1 < NKT:
                                stA_P(ti + 1)
                            stA_Ra(ti)
                            if ti >= 1:
                                stA_Rb(ti - 1)
                        stA_Rb(NKT - 1)
                        fw.barrier()
                    if DEBUG_STOP == "A":
                        return
                    with ExitStack() as sb_:
                        EB = sb_.enter_context
                        cs_t = EB(SB("ropecs", [128, L], F32))
                        b_cs = Buf("ropecs")
                        for j in range(2):
                            sp.dma(cs_t[64 * j:64 * j + 64, :], rope_cs_d[j, 0:64, :], writes=[b_cs])
                        wq_t = EB(SB("wq", [128, 2, 4, DN + DR], BF16))
                        wqr_ = Ring([wq_t[:, j] for j in range(2)], "wq")
                        wrot_t = EB(SB("wrot", [128, 2, 4, 2 * DR], BF16))
                        wrotr = Ring([wrot_t[:, j] for j in range(2)], "wrot")
                        wkv_t = EB(SB("wkv", [128, 2, 4, DN + DV], BF16))
                        wkvr = Ring([wkv_t[:, j] for j in range(2)], "wkv")
                        KT_t = EB(SB("KT", [128, 2, NTOK], BF16))
                        KTr = Ring([KT_t[:, j] for j in range(2)], "KT")
                        V_t = EB(SB("V", [128, 2, NKT, DV], BF16))
                        Vr = Ring([V_t[:, j] for j in range(2)], "V")
                        qn_t = EB(SB("qnT", [128, 2, L], BF16))
                        qnr = Ring([qn_t[:, j] for j in range(2)], "qnT")
                        qr_t = EB(SB("qrT", [128, 2, L], BF16))
                        qrr = Ring([qr_t[:, j] for j in range(2)], "qrT")
                        pT_t = EB(SB("pT", [128, 6, 512], BF16))
                        pTr = Ring([pT_t[:, j] for j in range(6)], "pT")
                        sq_t = EB(SB("sq", [128, NTOK], BF16))
                        b_sq = Buf("sq")
                        krsq = EB(SB("krsq", [128, NTOK], BF16))
                        b_krsq = Buf("krsq")
                        qrsq = EB(SB("qrsq", [128, L], BF16))
                        b_qrsq = Buf("qrsq")
                        rden_t = EB(SB("rden", [128, 2, 512], F32))
                        rdenr = Ring([rden_t[:, j] for j in range(2)], "rden")
                        mx_t = EB(SB("mx", [128, 16], F32))
                        b_mx = Buf("mx")
                        nb_t = EB(SB("nbias", [128, 2, 4], F32))
                        nbr = Ring([nb_t[:, j] for j in range(2)], "nbias")
                        pool.op(lambda: nc.gpsimd.memset(krsq[64:128, :], 0.0), writes=[b_krsq])
                        pool.op(lambda: nc.gpsimd.tensor_tensor(out=krsq[0:64, :], in0=kr2T[0:64, :], in1=kr2T[0:64, :], op=ALU.mult),
                                reads=[b_kr2], writes=[b_krsq])
                        kblocks = col_blocks(NTOK)
                        accs = [(pall.slots[4], pall.slots[5]), (pall.slots[6], pall.slots[7])]
                        ring4 = pring4
                        hb = {}
                        sbank = {}

                        def live_banks():
                            return [b for (_, b) in sbank.values()]

                        wts = {}

                        def load_head_weights(h):
                            wq, wq_b = wqr_.next()
                            wrot, wrot_b = wrotr.next()
                            wkv, wkv_b = wkvr.next()
                            pool.dma(wq, mla_w_uq[:, h * (DN + DR):(h + 1) * (DN + DR)].rearrange("(c p) f -> p c f", p=128), writes=[wq_b])
                            pool.dma(wkv, mla_w_ukv[:, h * (DN + DV):(h + 1) * (DN + DV)].rearrange("(c p) f -> p c f", p=128), writes=[wkv_b])
                            wqv = wq[:, :, DN:DN + DR].rearrange("p c (h j n) -> p c h j n", h=2, j=2)
                            wrv = wrot[:, :, DR:2 * DR].rearrange("p c (h j n) -> p c h j n", h=2, j=2)
                            pool.op(lambda: nc.gpsimd.tensor_copy(out=wrot[:, :, 0:DR], in_=wq[:, :, DN:DN + DR]), reads=[wq_b], writes=[wrot_b])
                            for hh in range(2):
                                pool.op(lambda hh=hh: nc.gpsimd.tensor_scalar(out=wrv[:, :, hh, 0, :], in0=wqv[:, :, hh, 1, :], scalar1=-1.0, scalar2=None, op0=ALU.mult),
                                        reads=[wq_b], writes=[wrot_b])
                                pool.op(lambda hh=hh: nc.gpsimd.tensor_copy(out=wrv[:, :, hh, 1, :], in_=wqv[:, :, hh, 0, :]), reads=[wq_b], writes=[wrot_b])
                            wts[h] = (wq, wq_b, wrot, wrot_b, wkv, wkv_b)

                        def prep_gen(h):
                                wq, wq_b, wrot, wrot_b, wkv, wkv_b = wts[h]
                                KT, KT_b = KTr.next()
                                V, V_b = Vr.next()
                                qn, qn_b = qnr.next()
                                qr, qr_b = qrr.next()
                                nb, nb_b = nbr.next()
                                for bi, (s_, sz) in enumerate(kblocks):
                                    pb, pb_b = ring4.next_skip(live_banks())
                                    for c in range(4):
                                        pe.op(lambda c=c, pb=pb, s_=s_, sz=sz: nc.tensor.matmul(
                                            pb[:, 0:sz], lhsT=wkv[:, c, 0:DN], rhs=ckvnT[:, c, s_:s_ + sz], start=(c == 0), stop=(c == 3)),
                                            reads=[wkv_b, b_ckvn], writes=[pb_b], signal=(c == 3))
                                    dve.op(lambda pb=pb, s_=s_, sz=sz: nc.vector.tensor_copy(out=KT[:, s_:s_ + sz], in_=pb[:, 0:sz]), reads=[pb_b], writes=[KT_b])
                                    yield
                                pool.op(lambda: nc.gpsimd.tensor_tensor(out=sq_t[:, 0:NTOK], in0=KT[:, :], in1=KT[:, :], op=ALU.mult), reads=[KT_b], writes=[b_sq])
                                for g4 in range(0, NKT, 4):
                                    nk = min(4, NKT - g4)
                                    pb, pb_b = ring4.next_skip(live_banks())
                                    for j in range(nk):
                                        kt = g4 + j
                                        for c in range(4):
                                            pe.op(lambda c=c, pb=pb, j=j, kt=kt: nc.tensor.matmul(
                                                pb[:, j * 128:(j + 1) * 128], lhsT=ckvnT[:, c, kt * 128:(kt + 1) * 128], rhs=wkv[:, c, DN:DN + DV],
                                                start=(c == 0), stop=(c == 3)),
                                                reads=[wkv_b, b_ckvn], writes=[pb_b], signal=(c == 3))
                                    dve.op(lambda pb=pb, g4=g4, nk=nk: nc.vector.tensor_copy(out=V[:, g4:g4 + nk, :], in_=pb[:, 0:nk * 128].rearrange("p (k d) -> p k d", d=DV)),
                                           reads=[pb_b], writes=[V_b])
                                    yield
                                for _ in range(2):
                                    yield
                                for bi, (s_, sz) in enumerate(kblocks):
                                    pb, pb_b = ring4.next_skip(live_banks())
                                    pe.op(lambda pb=pb, s_=s_, sz=sz: nc.tensor.matmul(pb[:, 0:sz], lhsT=ones_bf[:, :], rhs=sq_t[:, s_:s_ + sz], start=True, stop=False),
                                          reads=[ones_b, b_sq], writes=[pb_b], signal=False)
                                    pe.op(lambda pb=pb, s_=s_, sz=sz: nc.tensor.matmul(pb[:, 0:sz], lhsT=ones_bf[:, :], rhs=krsq[:, s_:s_ + sz], start=False, stop=True),
                                          reads=[ones_b, b_krsq], writes=[pb_b])
                                    dve.op(lambda pb=pb, sz=sz, bi=bi: nc.vector.tensor_reduce(out=mx_t[:, bi:bi + 1], in_=pb[:, 0:sz], axis=mybir.AxisListType.X, op=ALU.max),
                                           reads=[pb_b], writes=[b_mx])
                                    yield
                                for qb in range(4):
                                    q0 = qb * 512
                                    pb, pb_b = ring4.next_skip(live_banks())
                                    for c in range(4):
                                        pe.op(lambda c=c, pb=pb, q0=q0: nc.tensor.matmul(
                                            pb[:, :], lhsT=wq[:, c, 0:DN], rhs=cqnT[:, c, q0:q0 + 512], start=(c == 0), stop=(c == 3)),
                                            reads=[wq_b, b_cqn], writes=[pb_b], signal=(c == 3))
                                    dve.op(lambda pb=pb, q0=q0: nc.vector.tensor_copy(out=qn[:, q0:q0 + 512], in_=pb[:, :]), reads=[pb_b], writes=[qn_b])
                                    yield
                                    p1, p1_b = ring4.next_skip(live_banks())
                                    for c in range(4):
                                        pe.op(lambda c=c, q0=q0: nc.tensor.matmul(
                                            p1[:, :], lhsT=wrot[:, c, :], rhs=cqnT[:, c, q0:q0 + 512], start=(c == 0), stop=(c == 3)),
                                            reads=[wrot_b, b_cqn], writes=[p1_b], signal=(c == 3))
                                    dve.op(lambda q0=q0: nc.vector.tensor_tensor(out=qr[:, q0:q0 + 512], in0=p1[:, :], in1=cs_t[:, q0:q0 + 512], op=ALU.mult),
                                           reads=[p1_b, b_cs], writes=[qr_b])
                                    yield
                                pool.op(lambda: nc.gpsimd.tensor_tensor(out=sq_t[:, 0:L], in0=qn[:, :], in1=qn[:, :], op=ALU.mult), reads=[qn_b], writes=[b_sq])
                                pool.op(lambda: nc.gpsimd.tensor_tensor(out=qrsq[:, :], in0=qr[:, :], in1=qr[:, :], op=ALU.mult),
                                        reads=[qr_b], writes=[b_qrsq])
                                qsq_r = qrsq
                                for _ in range(9):
                                    yield
                                for qb in range(4):
                                    q0 = qb * 512
                                    pb, pb_b = ring4.next_skip(live_banks())
                                    pe.op(lambda pb=pb, q0=q0: nc.tensor.matmul(pb[:, :], lhsT=ones_bf[:, :], rhs=sq_t[:, q0:q0 + 512], start=True, stop=False),
                                          reads=[ones_b, b_sq], writes=[pb_b], signal=False)
                                    pe.op(lambda pb=pb, q0=q0: nc.tensor.matmul(pb[:, :], lhsT=ones_bf[:, :], rhs=qsq_r[:, q0:q0 + 512], start=False, stop=True),
                                          reads=[ones_b, b_qrsq], writes=[pb_b])
                                    dve.op(lambda pb=pb, qb=qb: nc.vector.tensor_reduce(out=mx_t[:, 8 + qb:9 + qb], in_=pb[:, :], axis=mybir.AxisListType.X, op=ALU.max),
                                           reads=[pb_b], writes=[b_mx])
                                    yield
                                nkb = len(kblocks)
                                dve.op(lambda: nc.vector.tensor_reduce(out=nb[:, 0:1], in_=mx_t[:, 0:nkb], axis=mybir.AxisListType.X, op=ALU.max), reads=[b_mx], writes=[nb_b])
                                dve.op(lambda: nc.vector.tensor_reduce(out=nb[:, 1:2], in_=mx_t[:, 8:12], axis=mybir.AxisListType.X, op=ALU.max), reads=[b_mx], writes=[nb_b])
                                dve.op(lambda: nc.vector.tensor_tensor(out=nb[:, 2:3], in0=nb[:, 0:1], in1=nb[:, 1:2], op=ALU.mult), reads=[nb_b], writes=[nb_b])
                                act.op(lambda: nc.scalar.activation(out=nb[:, 2:3], in_=nb[:, 2:3], func=AF.Sqrt), reads=[nb_b], writes=[nb_b])
                                dve.op(lambda: nc.vector.tensor_scalar(out=nb[:, 3:4], in0=nb[:, 2:3], scalar1=-MLA_SCALE, scalar2=None, op0=ALU.mult), reads=[nb_b], writes=[nb_b])

                                if h + 1 < H:
                                    load_head_weights(h + 1)
                                hb[h] = dict(KT=KT, KT_b=KT_b, V=V, V_b=V_b, qn=qn, qn_b=qn_b, qr=qr, qr_b=qr_b, nb=nb, nb_b=nb_b)

                        items = [(h, qb, kt) for h in range(H) for qb in range(4) for kt in range(NKT)]

                        def S_emit(i):
                            h, qb, kt = items[i]
                            B = hb[h]
                            q0 = qb * 512
                            pb, pb_b = ring4.next_skip(live_banks())
                            pe.op(lambda: nc.tensor.matmul(pb[:, :], lhsT=B["KT"][:, kt * 128:(kt + 1) * 128], rhs=B["qn"][:, q0:q0 + 512], start=True, stop=False),
                                  reads=[B["KT_b"], B["qn_b"]], writes=[pb_b], signal=False)
                            pe.op(lambda: nc.tensor.matmul(pb[:, :], lhsT=kr2T[:, kt * 128:(kt + 1) * 128], rhs=B["qr"][:, q0:q0 + 512], start=False, stop=True),
                                  reads=[b_kr2, B["qr_b"]], writes=[pb_b])
                            sbank[i] = (pb, pb_b)

                        AHEAD = 2
                        load_head_weights(0)
                        for _ in prep_gen(0):
                            pass
                        pgen = None
                        for i in range(min(AHEAD, len(items))):
                            S_emit(i)
                        for i, (h, qb, kt) in enumerate(items):
                            if qb == 0 and kt == 4 and h + 1 < H:
                                pgen = prep_gen(h + 1)
                            if pgen is not None:
                                if i + AHEAD < len(items) and items[i + AHEAD][0] != h:
                                    for _ in pgen:
                                        pass
                                    pgen = None
                                elif next(pgen, "done") == "done":
                                    pgen = None
                            if i + AHEAD < len(items):
                                S_emit(i + AHEAD)
                            B = hb[h]
                            q0 = qb * 512
                            (acc_o, acc_o_b), (acc_d, acc_d_b) = accs[(h * 4 + qb) % 2]
                            pb, pb_b = sbank.pop(i)
                            pt, pt_b = pTr.next()
                            act.op(lambda: nc.scalar.activation(out=pt, in_=pb[:, :], func=AF.Exp, scale=MLA_SCALE, bias=B["nb"][:, 3:4]),
                                   reads=[pb_b, B["nb_b"]], writes=[pt_b])
                            pe.op(lambda: nc.tensor.matmul(acc_o[:, :], lhsT=B["V"][:, kt, :], rhs=pt, start=(kt == 0), stop=(kt == NKT - 1)),
                                  reads=[B["V_b"], pt_b], writes=[acc_o_b], signal=False)
                            pe.op(lambda: nc.tensor.matmul(acc_d[:, :], lhsT=ones_bf[:, :], rhs=pt, start=(kt == 0), stop=(kt == NKT - 1)),
                                  reads=[ones_b, pt_b], writes=[acc_d_b])
                            if kt == NKT - 1:
                                rd, rd_b = rdenr.next()
                                dve.op(lambda: nc.vector.reciprocal(out=rd, in_=acc_d[:, :]), reads=[acc_d_b], writes=[rd_b])
                                dve.op(lambda: nc.vector.tensor_tensor(out=oT[:, h, q0:q0 + 512], in0=acc_o[:, :], in1=rd, op=ALU.mult),
                                       reads=[acc_o_b, rd_b], writes=[b_oT[h]])
                        fw.barrier()
                if DEBUG_STOP == "B":
                    return
                with ExitStack() as sc:
                    EC = sc.enter_context
                    junk = None
                    wout = EC(SB("wout1", [128, KC, D], BF16))
                    b_wout = Buf("wout1")
                    wout_todo1 = list(range(4))

                    def load_wout1_panel():
                        if wout_todo1:
                            n = wout_todo1.pop(0)
                            pool.dma(wout[:, :, n * 512:(n + 1) * 512], mla_w_out[:, n * 512:(n + 1) * 512].rearrange("(kc p) n -> p kc n", p=128), writes=[b_wout])
                    CW = 512
                    NSC = L // CW
                    hC_t = EC(SB("hC", [128, 2, KC, CW], BF16))
                    b_hC = [Buf("hC0"), Buf("hC1")]
                    wg_t = EC(SB("wgC", [128, 2, KC, 128], BF16))
                    wgr = Ring([wg_t[:, j] for j in range(2)], "wgC")
                    sg_t = EC(SB("sgC", [128, 2, CW], BF16))
                    sgr = Ring([sg_t[:, j] for j in range(2)], "sgC")
                    big_t = EC(SB("bigC", [128, 3, D], F32))
                    big = Ring([big_t[:, j] for j in range(3)], "bigC")
                    xn_t = EC(SB("xnC", [128, 1, D], BF16))
                    xnr = Ring([xn_t[:, 0]], "xnC")
                    b_oTc = [[Buf(f"oTc{ch}_{i}") for i in range(NSC)] for ch in range(H)]
                    gcol0 = QL + KVL + DR

                    def normC_A(j, tt):
                        row = LC + j * CW + tt * 128
                        return norm_tile_A(xc1[row:row + 128, :], xc1_buf, 128, big.next(), xnr.next(), smalls.next())

                    def normC_B(j, tt, stt):
                        norm_tile_B(stt, 0, lambda fc: hC_t[:, j % 2, fc, tt * 128:(tt + 1) * 128], b_hC[j % 2])

                    def stage_G_pair(i0):
                        for ch in range(H):
                            wgc, wgc_b = wgr.next()
                            pool.dma(wgc, mla_w_in[:, gcol0 + ch * 128: gcol0 + (ch + 1) * 128].rearrange("(kc p) f -> p kc f", p=128), writes=[wgc_b])
                            if ch % 2 == 1:
                                load_wout1_panel()
                            for i in (i0, i0 + 1):
                                t0 = i * CW
                                pb, pb_b = pring.next()
                                for kc in range(KC):
                                    pe.op(lambda kc=kc: nc.tensor.matmul(pb[:, 0:CW], lhsT=wgc[:, kc, :], rhs=hC_t[:, i % 2, kc, :], start=(kc == 0), stop=(kc == KC - 1)),
                                          reads=[wgc_b, b_hC[i % 2]], writes=[pb_b], signal=(kc == KC - 1))
                                sgc, sgc_b = sgr.next()
                                act.op(lambda: nc.scalar.activation(out=sgc, in_=pb[:, 0:CW], func=AF.Silu), reads=[pb_b], writes=[sgc_b])
                                dve.op(lambda: nc.vector.tensor_tensor(out=oT[:, ch, t0:t0 + CW], in0=oT[:, ch, t0:t0 + CW], in1=sgc, op=ALU.mult),
                                       reads=[sgc_b, b_oTc[ch][i]], writes=[b_oTc[ch][i]])

                    def stage_O_pair(i0):
                        nxt = [(j, tt) for j in (i0 + 2, i0 + 3) if j < NSC for tt in range(CW // 128)]
                        for i in (i0, i0 + 1):
                            for tt in range(CW // 128):
                                tk = i * CW + tt * 128
                                row = LC + tk
                                nt = nxt.pop(0) if nxt else None
                                stt = normC_A(*nt) if nt else None
                                out_proj_tile(lambda kc, tk=tk: oT[:, kc, tk:tk + 128], [b_oTc[ch][i] for ch in range(H)], wout, b_wout, 0,
                                              xc1[row:row + 128, :], xc1_buf, outd[tk:tk + 128, :], out_buf, big, smalls, junk)
                                if nt:
                                    normC_B(nt[0], nt[1], stt)

                    first = [(j, tt) for j in (0, 1) for tt in range(CW // 128)]
                    for nt in first:
                        normC_B(nt[0], nt[1], normC_A(*nt))
                    for i0 in range(0, NSC, 2):
                        stage_G_pair(i0)
                        stage_O_pair(i0)
                    fw.barrier()

        if do_l0:
            layer0()
        if do_l1:
            layer1()
        fw.barrier()
    return nc


def _consts():
    inv_edge = np.zeros((4, 16), np.float32)
    for g, w in enumerate(POOL_WINDOWS):
        left = w // 2
        for t in range(8):
            inv_edge[g, t] = 1.0 / min(w, t + w - left)
            tr = 8 - t
            inv_edge[g, 8 + t] = 1.0 / min(w, tr + left)
    n = DR // 4
    freqs = (np.float32(10000.0) ** (-np.arange(n, dtype=np.float32) / np.float32(n))).astype(np.float32)
    t = np.arange(L)
    row = (t // GRID_W).astype(np.float32)
    col = (t % GRID_W).astype(np.float32)
    ang = np.stack([row[:, None] * freqs[None, :], col[:, None] * freqs[None, :]], axis=1).astype(np.float32)
    cos, sin = np.cos(ang).astype(np.float32), np.sin(ang).astype(np.float32)
    rope_tok = np.stack([cos.reshape(L, 2 * n), sin.reshape(L, 2 * n)], axis=1)
    cosT = np.repeat(cos[:, :, None, :], 2, axis=2).reshape(L, DR).T
    sinT = np.repeat(sin[:, :, None, :], 2, axis=2).reshape(L, DR).T
    rope_cs = np.stack([np.concatenate([cosT, cosT], 0), np.concatenate([sinT, sinT], 0)], 0)
    return dict(ident=np.eye(128, dtype=np.float32), inv_edge=inv_edge,
                rope_tok=np.ascontiguousarray(rope_tok, dtype=np.float32), rope_cs=np.ascontiguousarray(rope_cs, dtype=np.float32))


_CACHE = {}


def _prog(stage):
    if stage not in _CACHE:
        _CACHE[stage] = build(stage)
    return _CACHE[stage]


def make_in_maps(inputs, stage, cores=8, xc1=None):
    f = lambda k: np.ascontiguousarray(np.asarray(inputs[k], dtype=np.float32))
    cst = _consts()
    x, c, ctx, c_ctx = f("x"), f("c"), f("ctx"), f("c_ctx")
    shared = dict(ada_w=f("ada_w"), ada_b=f("ada_b"), pre_norm=f("pre_norm"), post_norm=f("post_norm"), ident=cst["ident"])
    if stage in ("l0", "fused"):
        shared.update(pool_w_in=f("pool_w_in")[0], pool_w_grp=f("pool_w_grp")[0], pool_b_grp=f("pool_b_grp")[0],
                      pool_scale=f("pool_scale")[0], pool_w_out=f("pool_w_out")[0], inv_edge=cst["inv_edge"])
    if stage in ("l1", "fused"):
        shared.update(mla_w_in=f("mla_w_in")[0], mla_q_norm=f("mla_q_norm")[0], mla_kv_norm=f("mla_kv_norm")[0],
                      mla_w_uq=f("mla_w_uq")[0], mla_w_ukv=f("mla_w_ukv")[0], mla_w_out=f("mla_w_out")[0],
                      rope_tok=cst["rope_tok"], rope_cs=cst["rope_cs"])
    in_maps = []
    for b in range(cores):
        m = dict(shared)
        if stage in ("l0", "fused"):
            m["xc"] = np.concatenate([ctx[b], x[b]], axis=0)
        if stage == "l1":
            m["xc1"] = xc1[b]
        m["cvec"] = np.stack([c[b], c_ctx], axis=0)
        in_maps.append(m)
    return in_maps


def run_l0(inputs, cores=8):
    res = run_bass_kernel_spmd(_prog("l0"), make_in_maps(inputs, "l0", cores), core_ids=list(range(cores)))
    return [r["xc1"] for r in res.results]


N_CORES = 8
MODE = "fused"


def kernel(**inputs):
    if MODE == "fused":
        res = run_bass_kernel_spmd(_prog("fused"), make_in_maps(inputs, "fused", N_CORES), core_ids=list(range(N_CORES)))
    else:
        r0 = run_bass_kernel_spmd(_prog("l0"), make_in_maps(inputs, "l0", N_CORES), core_ids=list(range(N_CORES)))
        xc1 = [np.asarray(r["xc1"]) for r in r0.results]
        res = run_bass_kernel_spmd(_prog("l1"), make_in_maps(inputs, "l1", N_CORES, xc1=xc1), core_ids=list(range(N_CORES)))
    return np.stack([np.asarray(r["out"], dtype=np.float32) for r in res.results], axis=0)
```

```python
import numpy as np
from contextlib import ExitStack
import concourse.bass as bass
import concourse.mybir as mybir
from concourse.bass_utils import run_bass_kernel_spmd

F32 = mybir.dt.float32
BF16 = mybir.dt.bfloat16
AF = mybir.ActivationFunctionType
ALU = mybir.AluOpType

D = 2048
KC = D // 128
LC = 256
L = 2048
NTOK = LC + L
EPS = 1e-6
POOL_WINDOWS = (2, 4, 8, 16)
H = 16
QL = 512
KVL = 512
DN = 128
DR = 64
DV = 128
MLA_IN = QL + KVL + DR + H * DV
MLA_SCALE = float((DN + DR) ** -0.5)
GRID_W = 64
DEBUG_STOP = None
DEBUG_LEVEL = 99


class Buf:
    __slots__ = ("name", "writers", "readers", "war", "excl", "last")

    def __init__(self, name="", excl=False):
        self.name = name
        self.writers = {}
        self.readers = {}
        self.war = {}
        self.excl = excl
        self.last = {}


def _merge(dst, src):
    for k, (sem, val) in src.items():
        if k not in dst or dst[k][1] < val:
            dst[k] = (sem, val)


class Q:
    def __init__(self, fw, eng, sem, dma_sems, name):
        self.fw, self.eng, self.sem, self.name = fw, eng, sem, name
        self.cnt = 0
        self.waited = {}
        self.dsems = [[s, 0] for s in dma_sems]
        self.rr = 0

    def _wait(self, deps):
        for k, (sem, val) in deps.items():
            if self.waited.get(k, 0) >= val:
                continue
            if k == self.sem.num and val > self.cnt:
                continue
            self.eng.wait_ge(sem, val)
            self.waited[k] = val

    def _deps(self, reads, writes):
        deps = {}
        for b in reads:
            if b.excl:
                _merge(deps, b.last)
                continue
            _merge(deps, b.writers)
        for b in writes:
            if b.excl:
                _merge(deps, b.last)
                continue
            if b.readers:
                b.war = b.readers
                b.readers = {}
                b.writers = {}
            _merge(deps, b.war)
        return deps

    def _record(self, tok, reads, writes):
        k = tok[0].num
        for b in list(reads) + list(writes):
            if b.excl and (k not in b.last or b.last[k][1] < tok[1]):
                b.last[k] = tok
        reads = [b for b in reads if not b.excl]
        writes = [b for b in writes if not b.excl]
        for b in reads:
            if k not in b.readers or b.readers[k][1] < tok[1]:
                b.readers[k] = tok
        for b in writes:
            if k not in b.writers or b.writers[k][1] < tok[1]:
                b.writers[k] = tok

    def op(self, fn, reads=(), writes=(), signal=True, extra=None):
        deps = self._deps(reads, writes)
        if extra:
            _merge(deps, extra)
        self._wait(deps)
        ins = fn()
        if signal:
            self.cnt += 1
            ins.then_inc(self.sem, 1)
            tok = (self.sem, self.cnt)
        else:
            tok = (self.sem, self.cnt + 1)
        self._record(tok, reads, writes)
        return tok

    def dma(self, out, in_, reads=(), writes=(), **kw):
        i = self.rr
        self.rr = (i + 1) % len(self.dsems)
        sem, cum = self.dsems[i]
        deps = self._deps(reads, writes)
        if cum:
            _merge(deps, {sem.num: (sem, cum)})
        self._wait(deps)
        self.eng.dma_start(out=out, in_=in_, **kw).then_inc(sem, 16)
        cum += 16
        self.dsems[i][1] = cum
        tok = (sem, cum)
        self._record(tok, reads, writes)
        return tok

    def all_tokens(self):
        d = {}
        if self.cnt:
            d[self.sem.num] = (self.sem, self.cnt)
        for sem, cum in self.dsems:
            if cum:
                d[sem.num] = (sem, cum)
        return d


class FW:
    def __init__(self, nc, es):
        self.nc = nc
        E = es.enter_context

        def sems(n, nm):
            return [E(nc.semaphore(f"{nm}{i}")) for i in range(n)]

        self.pe = Q(self, nc.tensor, E(nc.semaphore("s_pe")), [], "pe")
        self.act = Q(self, nc.scalar, E(nc.semaphore("s_act")), [], "act")
        self.dve = Q(self, nc.vector, E(nc.semaphore("s_dve")), [], "dve")
        self.pool = Q(self, nc.gpsimd, E(nc.semaphore("s_pool")), sems(20, "dq_pool"), "pool")
        self.sp = Q(self, nc.sync, E(nc.semaphore("s_sp")), sems(20, "dq_sp"), "sp")
        self.qs = [self.pe, self.act, self.dve, self.pool, self.sp]

    def barrier(self):
        allt = {}
        for q in self.qs:
            _merge(allt, q.all_tokens())
        for q in self.qs:
            q._wait(allt)


class Ring:
    def __init__(self, aps, name, excl=False):
        self.slots = [(ap, Buf(f"{name}{i}", excl)) for i, ap in enumerate(aps)]
        self.i = 0

    @classmethod
    def from_slots(cls, slots):
        r = cls([], "")
        r.slots = list(slots)
        return r

    def next_skip(self, live):
        for _ in range(len(self.slots)):
            s = self.next()
            if s[1] not in live:
                return s
        raise RuntimeError("no free slot")

    def next(self):
        s = self.slots[self.i]
        self.i = (self.i + 1) % len(self.slots)
        return s


def col_blocks(n, maxb=512):
    nb = (n + maxb - 1) // maxb
    base = n // nb
    rem = n - base * nb
    out = []
    s = 0
    for i in range(nb):
        sz = base + (1 if i < rem else 0)
        out.append((s, sz))
        s += sz
    return out


def build(stage):
    nc = bass.Bass("TRN2", target_bir_lowering=False)
    es = ExitStack()
    with es:
        E = es.enter_context
        fw = FW(nc, es)
        pe, act, dve, pool, sp = fw.pe, fw.act, fw.dve, fw.pool, fw.sp
        _uid = [0]

        def SB(name, shape, dt):
            _uid[0] += 1
            return nc.sbuf_tensor(f"sb_{name}_{_uid[0]}", shape, dt)

        def PS(name, shape, dt):
            _uid[0] += 1
            return nc.psum_tensor(f"ps_{name}_{_uid[0]}", shape, dt)

        def din(name, shape):
            return nc.dram_tensor(name, list(shape), F32, kind="ExternalInput").ap()

        cvec = din("cvec", (2, D))
        ada_w = din("ada_w", (2, D, 3 * D))
        ada_b = din("ada_b", (2, 3 * D))
        pre_norm = din("pre_norm", (2, D))
        post_norm = din("post_norm", (2, D))
        ident_d = din("ident", (128, 128))
        do_l0 = stage in ("l0", "fused")
        do_l1 = stage in ("l1", "fused")
        if do_l0:
            xc = din("xc", (NTOK, D))
            pool_w_in = din("pool_w_in", (D, 2 * D))
            pool_w_grp = din("pool_w_grp", (4, 512, 512))
            pool_b_grp = din("pool_b_grp", (4, 512))
            pool_scale = din("pool_scale", (D,))
            pool_w_out = din("pool_w_out", (D, D))
            inv_edge_d = din("inv_edge", (4, 16))
        if do_l1:
            mla_w_in = din("mla_w_in", (D, MLA_IN))
            mla_q_norm = din("mla_q_norm", (QL,))
            mla_kv_norm = din("mla_kv_norm", (KVL,))
            mla_w_uq = din("mla_w_uq", (QL, H * (DN + DR)))
            mla_w_ukv = din("mla_w_ukv", (KVL, H * (DN + DV)))
            mla_w_out = din("mla_w_out", (H * DV, D))
            rope_cs_d = din("rope_cs", (2, 128, L))
            rope_tok_d = din("rope_tok", (L, 2, 32))
        if stage == "l0":
            xc1 = nc.dram_tensor("xc1", [NTOK, D], F32, kind="ExternalOutput").ap()
        elif stage == "l1":
            xc1 = din("xc1", (NTOK, D))
        else:
            xc1 = nc.dram_tensor("xc1", [NTOK, D], F32).ap()
        xc1_buf = Buf("xc1")
        out_buf = Buf("out")
        if do_l1:
            outd = nc.dram_tensor("out", [L, D], F32, kind="ExternalOutput").ap()

        ident = E(SB("ident", [128, 128], BF16))
        ident_b = Buf("ident")
        pool.dma(ident[:], ident_d[:, :], writes=[ident_b])
        identf = E(SB("identf", [32, 32], F32))
        identf_b = Buf("identf")
        sp.dma(identf[:], ident_d[0:32, 0:32], writes=[identf_b])
        ones_bf = E(SB("ones_bf", [128, 128], BF16))
        ones_b = Buf("ones")
        dve.op(lambda: nc.vector.memset(ones_bf[:], 1.0), writes=[ones_b])

        pbanks = [E(PS(f"pb{i}", [128, 512], F32)) for i in range(8)]
        pall = Ring([p for p in pbanks], "pb", excl=True)
        pring6 = Ring.from_slots(pall.slots[0:6])
        pring4 = Ring.from_slots(pall.slots[0:4])
        pring = pring6

        class _PTR:
            def __getitem__(self, key):
                p, half, cols = key
                return pbanks[6 + half][:, :].bitcast(BF16)[p, cols]
        ptr = _PTR()
        ptr_bufs = [pall.slots[6][1], pall.slots[7][1]]

        rows_t = E(SB("rows_t", [32, 128], F32))
        rows_b = Buf("rows")

        def load_cols(dst, dst_buf, src_rows, n):
            sp.dma(rows_t[:n, :], src_rows, writes=[rows_b])
            pb, pb_b = pring.next()
            pe.op(lambda: nc.tensor.transpose(pb[:, 0:n], rows_t[:n, :], identf[:n, :n]), reads=[rows_b, identf_b], writes=[pb_b])
            dve.op(lambda: nc.vector.tensor_copy(out=dst, in_=pb[:, 0:n]), reads=[pb_b], writes=[dst_buf])

        A_col = E(SB("A_col", [128, KC, 2], F32))
        Sh_col = E(SB("Sh_col", [128, KC, 2], F32))
        G_box = [None]
        mod_b = Buf("mod")
        G_b = [Buf("G0"), Buf("G1")]

        def ada_layer(i, want_ctx_gate):
            with ExitStack() as s2:
                E2 = s2.enter_context
                csb = E2(SB("csb", [128, 2, KC], F32))
                sT = E2(SB("sT", [128, 2, KC], BF16))
                sT_rep = E2(SB("sT_rep", [128, 2, KC, 128], BF16))
                bcol = E2(SB("bcol", [128, 32], F32))
                pcol = E2(SB("pcol", [128, KC], F32))
                modc = E2(SB("modc", [128, 32, 2], F32))
                bg_rep = E2(SB("bg_rep", [128, D], F32))
                pn_rep = E2(SB("pn_rep", [128, D], F32))
                wblk_t = E2(SB("wblk", [128, 6, KC, 512], BF16))
                wring = Ring([wblk_t[:, j] for j in range(6)], "wblk")
                b_csb, b_sT, b_rep, b_bcol, b_pcol, b_modc, b_bg, b_pn = (Buf(n) for n in
                                                                          ("csb", "sT", "rep", "bcol", "pcol", "modc", "bg", "pn"))
                load_cols(csb[:].rearrange("p r k -> p (r k)"), b_csb, cvec.rearrange("r (kc p) -> (r kc) p", p=128), 32)
                load_cols(bcol[:], b_bcol, ada_b[i, 0:2 * D].rearrange("(c p) -> c p", p=128), 32)
                load_cols(pcol[:], b_pcol, pre_norm[i].rearrange("(c p) -> c p", p=128), 16)
                sp.dma(bg_rep[:], ada_b[i:i + 1, 2 * D:3 * D].partition_broadcast(128), writes=[b_bg])
                sp.dma(pn_rep[:], post_norm[i:i + 1, :].partition_broadcast(128), writes=[b_pn])
                act.op(lambda: nc.scalar.activation(out=sT[:], in_=csb[:], func=AF.Silu), reads=[b_csb], writes=[b_sT])
                for r in range(2):
                    dve.op(lambda r=r: nc.vector.tensor_copy(out=sT_rep[:, r], in_=sT[:, r, :].unsqueeze(2).to_broadcast([128, KC, 128])),
                           reads=[b_sT], writes=[b_rep])
                pm, pm_b = pring.next()
                pmv = pm[:, 0:64].rearrange("p (c r) -> p c r", r=2)
                for blk in range(12):
                    wb, wb_b = wring.next()
                    pool.dma(wb, ada_w[i][:, blk * 512:(blk + 1) * 512].rearrange("(kc p) n -> p kc n", p=128), writes=[wb_b])
                    if blk < 8:
                        for fc in range(4):
                            for kc in range(KC):
                                pe.op(lambda fc=fc, kc=kc, wb=wb, blk=blk: nc.tensor.matmul(
                                    pmv[:, blk * 4 + fc, :], lhsT=wb[:, kc, fc * 128:(fc + 1) * 128], rhs=sT[:, :, kc],
                                    start=(kc == 0), stop=(kc == KC - 1)),
                                    reads=[wb_b, b_sT], writes=[pm_b], signal=(kc == KC - 1))
                        if blk == 7:
                            dve.op(lambda: nc.vector.tensor_tensor(out=modc[:], in0=pmv, in1=bcol[:].unsqueeze(2).to_broadcast([128, 32, 2]),
                                                                   op=ALU.add), reads=[pm_b, b_bcol], writes=[b_modc])
                            dve.op(lambda: nc.vector.tensor_copy(out=Sh_col[:], in_=modc[:, 0:KC, :]), reads=[b_modc], writes=[mod_b])
                            dve.op(lambda: nc.vector.tensor_scalar(out=modc[:, KC:2 * KC, :], in0=modc[:, KC:2 * KC, :], scalar1=1.0,
                                                                   scalar2=None, op0=ALU.add), reads=[b_modc], writes=[b_modc])
                            dve.op(lambda: nc.vector.tensor_tensor(out=A_col[:], in0=modc[:, KC:2 * KC, :],
                                                                   in1=pcol[:].unsqueeze(2).to_broadcast([128, KC, 2]), op=ALU.mult),
                                   reads=[b_modc, b_pcol], writes=[mod_b])
                    else:
                        c0 = (blk - 8) * 512
                        for r in range(2 if want_ctx_gate else 1):
                            pg, pg_b = pring.next()
                            for kc in range(KC):
                                pe.op(lambda r=r, kc=kc, wb=wb, pg=pg: nc.tensor.matmul(
                                    pg[:], lhsT=sT_rep[:, r, kc, :], rhs=wb[:, kc, :], start=(kc == 0), stop=(kc == KC - 1)),
                                    reads=[wb_b, b_rep], writes=[pg_b], signal=(kc == KC - 1))
                            dve.op(lambda r=r, pg=pg, c0=c0: nc.vector.tensor_tensor(out=G_box[0][:, r, c0:c0 + 512], in0=pg[:],
                                                                                 in1=bg_rep[:, c0:c0 + 512], op=ALU.add),
                                   reads=[pg_b, b_bg], writes=[G_b[r]])
                            dve.op(lambda r=r, c0=c0: nc.vector.tensor_tensor(out=G_box[0][:, r, c0:c0 + 512], in0=G_box[0][:, r, c0:c0 + 512],
                                                                          in1=pn_rep[:, c0:c0 + 512], op=ALU.mult),
                                   reads=[G_b[r], b_pn], writes=[G_b[r]])
                fw.barrier()

        def norm_tile_A(src_ap, src_buf, nrows, xt_slot, xn_slot, small):
            xt, xt_b = xt_slot
            xn, xn_b = xn_slot
            (ss, rstd, _), sm_b = small
            sp.dma(xt[:nrows, :], src_ap, reads=[src_buf], writes=[xt_b])
            act.op(lambda: nc.scalar.activation(out=xn[:nrows, :], in_=xt[:nrows, :], func=AF.Square, accum_out=ss[:nrows, :]),
                   reads=[xt_b], writes=[sm_b, xn_b])
            act.op(lambda: nc.scalar.activation(out=rstd[:nrows, :], in_=ss[:nrows, :], func=AF.Sqrt, scale=1.0 / D, bias=eps_col[:nrows, :]),
                   reads=[sm_b, eps_b], writes=[sm_b])
            dve.op(lambda: nc.vector.reciprocal(out=rstd[:nrows, :], in_=rstd[:nrows, :]), reads=[sm_b], writes=[sm_b])
            dve.op(lambda: nc.vector.tensor_scalar(out=xn[:nrows, :], in0=xt[:nrows, :], scalar1=rstd[:nrows, :], scalar2=None, op0=ALU.mult),
                   reads=[xt_b, sm_b], writes=[xn_b])
            return (xn, xn_b, nrows)

        def norm_tile_B(state, seg, hT_dst, hT_buf):
            xn, xn_b, nrows = state
            for half in range(2):
                for j in range(8):
                    fc = half * 8 + j
                    pe.op(lambda fc=fc, half=half, j=j: nc.tensor.transpose(ptr[:, half, j * 128:j * 128 + nrows],
                                                                          xn[:nrows, fc * 128:(fc + 1) * 128], ident[:nrows, :nrows]),
                          reads=[xn_b, ident_b], writes=[ptr_bufs[half]], signal=(j == 7))
                for j in range(8):
                    fc = half * 8 + j
                    if half == 0:
                        act.op(lambda fc=fc, half=half, j=j: nc.scalar.activation(
                            out=hT_dst(fc), in_=ptr[:, half, j * 128:j * 128 + nrows], func=AF.Identity,
                            scale=A_col[:, fc, seg:seg + 1], bias=Sh_col[:, fc, seg:seg + 1]),
                            reads=[ptr_bufs[half], mod_b], writes=[hT_buf])
                    else:
                        dve.op(lambda fc=fc, half=half, j=j: nc.vector.tensor_scalar(
                            out=hT_dst(fc), in0=ptr[:, half, j * 128:j * 128 + nrows], scalar1=A_col[:, fc, seg:seg + 1],
                            scalar2=Sh_col[:, fc, seg:seg + 1], op0=ALU.mult, op1=ALU.add),
                            reads=[ptr_bufs[half], mod_b], writes=[hT_buf])

        def norm_tile_to_hT(src_ap, src_buf, nrows, seg, xt_slot, xn_slot, small, hT_dst, hT_buf, junk):
            norm_tile_B(norm_tile_A(src_ap, src_buf, nrows, xt_slot, xn_slot, small), seg, hT_dst, hT_buf)

        def out_proj_tile(lhsT_fn, lhs_buf, wout, b_wout, seg, x_src, x_src_buf, dst, dst_buf, big, smalls, junk):
            lhs_bufs = lhs_buf if isinstance(lhs_buf, list) else [lhs_buf] * KC
            xt, xt_b = big.next()
            ot, ot_b = big.next()
            sp.dma(xt[:], x_src, reads=[x_src_buf], writes=[xt_b])
            (ss, rstd, ss4), sm_b = smalls.next()
            banks = []
            for n in range(4):
                pb, pb_b = pring.next()
                banks.append((pb, pb_b))
                for kc in range(KC):
                    pe.op(lambda kc=kc, pb=pb, n=n: nc.tensor.matmul(
                        pb[:], lhsT=lhsT_fn(kc), rhs=wout[:, kc, n * 512:(n + 1) * 512],
                        start=(kc == 0), stop=(kc == KC - 1)),
                        reads=[lhs_bufs[kc], b_wout], writes=[pb_b], signal=(kc == KC - 1))
                act.op(lambda pb=pb, n=n, ss4=ss4: nc.scalar.activation(
                    out=ot[:, n * 512:(n + 1) * 512], in_=pb[:], func=AF.Square, accum_out=ss4[:, n:n + 1]),
                    reads=[pb_b], writes=[sm_b, ot_b])
                dve.op(lambda pb=pb, n=n: nc.vector.tensor_tensor(
                    out=ot[:, n * 512:(n + 1) * 512], in0=pb[:], in1=G_box[0][:, seg, n * 512:(n + 1) * 512], op=ALU.mult),
                    reads=[pb_b, G_b[seg]], writes=[ot_b])
            dve.op(lambda: nc.vector.tensor_reduce(out=ss, in_=ss4, axis=mybir.AxisListType.X, op=ALU.add),
                   reads=[sm_b], writes=[sm_b])
            act.op(lambda: nc.scalar.activation(out=rstd, in_=ss, func=AF.Sqrt, scale=1.0 / D, bias=eps_col[:]),
                   reads=[sm_b, eps_b], writes=[sm_b])
            dve.op(lambda: nc.vector.reciprocal(out=rstd, in_=rstd), reads=[sm_b], writes=[sm_b])
            dve.op(lambda: nc.vector.scalar_tensor_tensor(out=ot[:], in0=ot[:], scalar=rstd, in1=xt[:], op0=ALU.mult, op1=ALU.add),
                   reads=[xt_b, ot_b, sm_b], writes=[ot_b])
            sp.dma(dst, ot[:], reads=[ot_b], writes=[dst_buf])

        eps_col = E(SB("eps_col", [128, 1], F32))
        eps_b = Buf("eps")
        dve.op(lambda: nc.vector.memset(eps_col[:], EPS), writes=[eps_b])

        def layer0():
            with SB("G_rows0", [128, 2, D], F32) as g_:
                G_box[0] = g_
                layer0_body()

        def layer0_body():
            ada_layer(0, True)
            if DEBUG_STOP == "ada":
                return
            def lat(o0, o1):
                c0 = max(o0 - 8, 0)
                c1 = min(o1 + 8, L)
                return dict(seg=0, row0=LC + c0, ncomp=c1 - c0, olo=o0 - c0, ohi=o1 - c0, ledge=(o0 == 0), redge=(o1 == L))
            ctxseg = dict(seg=1, row0=0, ncomp=LC, olo=0, ohi=LC, ledge=True, redge=True)
            supers = [[ctxseg, lat(0, 384)], [lat(384, 896)], [lat(896, 1408)], [lat(1408, 2048)]]
            MAXC = 656
            with ExitStack() as s2:
                E2 = s2.enter_context
                hT = E2(SB("hT", [128, KC, MAXC], BF16))
                zT = E2(SB("zT", [128, KC, 640], BF16))
                wout = E2(SB("wout", [128, KC, D], BF16))
                wg = E2(SB("wg", [128, 4, 4, 512], BF16))
                ubuf_t = E2(SB("ubuf", [128, 2, MAXC + 32], F32))
                tmp_t = E2(SB("ptmp", [128, 2, MAXC + 32], F32))
                pT = E2(SB("pT", [128, 4, 640], BF16))
                sgT_t = E2(SB("sgT", [128, 2, MAXC], BF16))
                t1_t = E2(SB("t1", [128, 2, 640], BF16))
                wch_t = E2(SB("wch", [128, 3, KC, 128], BF16))
                big_t = E2(SB("big", [128, 3, D], F32))
                xn_t = E2(SB("xn", [128, 2, D], BF16))
                small_t = E2(SB("small", [128, 4, 8], F32))
                bgcol = E2(SB("bgcol", [128, 16], F32))
                pscol = E2(SB("pscol", [128, 16], F32))
                bscol = E2(SB("bscol", [128, 16], F32))
                inv_edge = E2(SB("inv_edge", [128, 4, 16], F32))
                edge_tmp = E2(SB("edge_tmp", [128, 16], F32))
                b_wout, b_wg, b_cols, b_inv, b_edge = Buf("wout"), Buf("wg"), Buf("cols"), Buf("inv"), Buf("edge")
                b_hT, b_zT, b_pT = Buf("hT"), Buf("zT"), [Buf(f"pT{c}") for c in range(4)]
                ubuf = Ring([ubuf_t[:, j] for j in range(2)], "ubuf")
                tmpb = [Buf("tmp0"), Buf("tmp1")]
                sgT = Ring([sgT_t[:, j] for j in range(2)], "sgT")
                t1r = Ring([t1_t[:, j] for j in range(2)], "t1")
                wch = Ring([wch_t[:, j] for j in range(3)], "wch")
                big = Ring([big_t[:, j] for j in range(3)], "big")
                xnr = Ring([xn_t[:, j] for j in range(2)], "xn")
                smalls = Ring([(small_t[:, j, 0:1], small_t[:, j, 1:2], small_t[:, j, 4:8]) for j in range(4)], "small")
                junk = None

                for j in range(2):
                    dve.op(lambda j=j: nc.vector.memset(ubuf_t[:, j], 0.0), writes=[ubuf.slots[j][1]])
                for g in range(4):
                    pool.dma(wg[:, g], pool_w_grp[g].rearrange("(c p) d -> p c d", p=128), writes=[b_wg])
                load_cols(bgcol[:], b_cols, pool_b_grp.rearrange("g (c p) -> (g c) p", p=128), 16)
                load_cols(pscol[:], b_cols, pool_scale.rearrange("(c p) -> c p", p=128), 16)
                sp.dma(inv_edge[:], inv_edge_d.rearrange("g e -> (g e)").partition_broadcast(128), writes=[b_inv])
                dve.op(lambda: nc.vector.tensor_tensor(out=bscol[:], in0=bgcol[:], in1=pscol[:], op=ALU.mult), reads=[b_cols], writes=[b_cols])
                wout_todo = list(range(4))

                def load_wout_panel():
                    if wout_todo:
                        n = wout_todo.pop(0)
                        pool.dma(wout[:, :, n * 512:(n + 1) * 512], pool_w_out[:, n * 512:(n + 1) * 512].rearrange("(kc p) n -> p kc n", p=128),
                                 writes=[b_wout])

                for st in supers:
                    c = 0
                    oc = 0
                    for sg in st:
                        sg["c0"] = c
                        sg["u0"] = c + 8 + 16 * st.index(sg)
                        sg["o0"] = oc
                        c += sg["ncomp"]
                        oc += sg["ohi"] - sg["olo"]

                def p1_tiles(st):
                    out = []
                    for sg in st:
                        r = 0
                        while r < sg["ncomp"]:
                            n = min(128, sg["ncomp"] - r)
                            out.append((sg, r, n))
                            r += n
                    return out

                def p1_A(tl):
                    sg, r, n = tl
                    return norm_tile_A(xc[sg["row0"] + r: sg["row0"] + r + n, :], Buf("xcin"), n, big.next(), xnr.next(), smalls.next())

                def p1_B(tl, stt):
                    sg, r, n = tl
                    col = sg["c0"] + r
                    norm_tile_B(stt, sg["seg"], lambda fc: hT[:, fc, col:col + n], b_hT)

                def p1_run(tiles):
                    if not tiles:
                        return
                    stt = p1_A(tiles[0])
                    for j, tl in enumerate(tiles):
                        nxt = p1_A(tiles[j + 1]) if j + 1 < len(tiles) else None
                        p1_B(tl, stt)
                        stt = nxt

                p1_run(p1_tiles(supers[0]))
                for si, st in enumerate(supers):
                    ncomp = sum(sg["ncomp"] for sg in st)
                    nout = sum(sg["ohi"] - sg["olo"] for sg in st)
                    for (ub_, ubb_) in ubuf.slots:
                        for sg in st:
                            if sg["ledge"]:
                                dve.op(lambda ub_=ub_, sg=sg: nc.vector.memset(ub_[:, sg["u0"] - 8: sg["u0"]], 0.0), writes=[ubb_])
                            if sg["redge"]:
                                dve.op(lambda ub_=ub_, sg=sg: nc.vector.memset(ub_[:, sg["u0"] + sg["ncomp"]: sg["u0"] + sg["ncomp"] + 8], 0.0),
                                       writes=[ubb_])
                    cblocks = []
                    for sg in st:
                        for (s, sz) in col_blocks(sg["ncomp"]):
                            cblocks.append((sg, s, sz))
                    oblocks = col_blocks(nout)
                    for g in range(4):
                        w = POOL_WINDOWS[g]
                        for cc in range(4):
                            j = 4 * g + cc
                            wc, wc_b = wch.next()
                            pool.dma(wc, pool_w_in[:, j * 128:(j + 1) * 128].rearrange("(kc p) f -> p kc f", p=128), writes=[wc_b])
                            ub, ub_b = ubuf.next()
                            for (sg, s, sz) in cblocks:
                                pb, pb_b = pring.next()
                                for kc in range(KC):
                                    pe.op(lambda kc=kc, wc=wc, pb=pb, sg=sg, s=s, sz=sz: nc.tensor.matmul(
                                        pb[:, 0:sz], lhsT=wc[:, kc, :], rhs=hT[:, kc, sg["c0"] + s: sg["c0"] + s + sz],
                                        start=(kc == 0), stop=(kc == KC - 1)),
                                        reads=[wc_b, b_hT], writes=[pb_b], signal=(kc == KC - 1))
                                act.op(lambda pb=pb, ub=ub, sg=sg, s=s, sz=sz: nc.scalar.copy(out=ub[:, sg["u0"] + s: sg["u0"] + s + sz], in_=pb[:, 0:sz]),
                                       reads=[pb_b], writes=[ub_b])
                            for sg in st:
                                NP = sg["ncomp"] + 16
                                b0 = sg["u0"] - 8
                                U = lambda a, b, ub=ub, b0=b0: ub[:, b0 + a: b0 + b]
                                T0 = lambda a, b, b0=b0: tmp_t[:, 0, b0 + a: b0 + b]
                                T1 = lambda a, b, b0=b0: tmp_t[:, 1, b0 + a: b0 + b]
                                dve.op(lambda U=U, T0=T0, NP=NP: nc.vector.tensor_tensor(out=T0(1, NP), in0=U(0, NP - 1), in1=U(1, NP), op=ALU.add),
                                       reads=[ub_b], writes=[tmpb[0]])
                                S, Sb = T0, tmpb[0]
                                if w >= 4:
                                    dve.op(lambda T0=T0, T1=T1, NP=NP: nc.vector.tensor_tensor(out=T1(2, NP - 1), in0=T0(3, NP), in1=T0(1, NP - 2), op=ALU.add),
                                           reads=[tmpb[0]], writes=[tmpb[1]])
                                    S, Sb = T1, tmpb[1]
                                if w >= 8:
                                    dve.op(lambda T0=T0, T1=T1, NP=NP: nc.vector.tensor_tensor(out=T0(4, NP - 3), in0=T1(6, NP - 1), in1=T1(2, NP - 5), op=ALU.add),
                                           reads=[tmpb[1]], writes=[tmpb[0]])
                                    S, Sb = T0, tmpb[0]
                                if w >= 16:
                                    dve.op(lambda T0=T0, T1=T1, NP=NP: nc.vector.tensor_tensor(out=T1(8, NP - 7), in0=T0(12, NP - 3), in1=T0(4, NP - 11), op=ALU.add),
                                           reads=[tmpb[0]], writes=[tmpb[1]])
                                    S, Sb = T1, tmpb[1]
                                lo, hi = 8 + sg["olo"], 8 + sg["ohi"]
                                no = hi - lo
                                o0 = sg["o0"]
                                dve.op(lambda S=S, U=U, lo=lo, hi=hi, o0=o0, no=no, cc=cc, w=w: nc.vector.scalar_tensor_tensor(
                                    out=pT[:, cc, o0:o0 + no], in0=S(lo, hi), scalar=1.0 / w, in1=U(lo, hi), op0=ALU.mult, op1=ALU.subtract),
                                    reads=[Sb, ub_b], writes=[b_pT[cc]])
                                for (is_edge, elo, tcol, ocol) in ((sg["ledge"], lo, 0, o0), (sg["redge"], hi - 8, 8, o0 + no - 8)):
                                    if not is_edge:
                                        continue
                                    dve.op(lambda S=S, elo=elo, tcol=tcol, g=g: nc.vector.tensor_tensor(
                                        out=edge_tmp[:, 0:8], in0=S(elo, elo + 8), in1=inv_edge[:, g, tcol:tcol + 8], op=ALU.mult),
                                        reads=[Sb, b_inv], writes=[b_edge])
                                    dve.op(lambda U=U, elo=elo, ocol=ocol, cc=cc: nc.vector.tensor_tensor(
                                        out=pT[:, cc, ocol:ocol + 8], in0=edge_tmp[:, 0:8], in1=U(elo, elo + 8), op=ALU.subtract),
                                        reads=[b_edge, ub_b], writes=[b_pT[cc]])
                        for dd in range(4):
                            jj = 16 + 4 * g + dd
                            zc = 4 * g + dd
                            wc, wc_b = wch.next()
                            pool.dma(wc, pool_w_in[:, jj * 128:(jj + 1) * 128].rearrange("(kc p) f -> p kc f", p=128), writes=[wc_b])
                            if dd == 3:
                                load_wout_panel()
                            sgt, sgt_b = sgT.next()
                            for sg in st:
                                olo, ohi = sg["olo"], sg["ohi"]
                                for (s, sz) in col_blocks(ohi - olo):
                                    pb, pb_b = pring.next()
                                    for kc in range(KC):
                                        pe.op(lambda kc=kc, wc=wc, pb=pb, sg=sg, s=s, sz=sz, olo=olo: nc.tensor.matmul(
                                            pb[:, 0:sz], lhsT=wc[:, kc, :], rhs=hT[:, kc, sg["c0"] + olo + s: sg["c0"] + olo + s + sz],
                                            start=(kc == 0), stop=(kc == KC - 1)),
                                            reads=[wc_b, b_hT], writes=[pb_b], signal=(kc == KC - 1))
                                    act.op(lambda pb=pb, sgt=sgt, sg=sg, s=s, sz=sz: nc.scalar.activation(
                                        out=sgt[:, sg["o0"] + s: sg["o0"] + s + sz], in_=pb[:, 0:sz], func=AF.Silu),
                                        reads=[pb_b], writes=[sgt_b])
                            t1, t1_b = t1r.next()
                            for (s, sz) in oblocks:
                                pb, pb_b = pring.next()
                                for cc in range(4):
                                    pe.op(lambda cc=cc, pb=pb, s=s, sz=sz, dd=dd, g=g: nc.tensor.matmul(
                                        pb[:, 0:sz], lhsT=wg[:, g, cc, dd * 128:(dd + 1) * 128], rhs=pT[:, cc, s:s + sz],
                                        start=(cc == 0), stop=(cc == 3)),
                                        reads=[b_wg, b_pT[cc]], writes=[pb_b], signal=(cc == 3))
                                act.op(lambda pb=pb, t1=t1, s=s, sz=sz, zc=zc: nc.scalar.activation(
                                    out=t1[:, s:s + sz], in_=pb[:, 0:sz], func=AF.Identity, scale=pscol[:, zc:zc + 1], bias=bscol[:, zc:zc + 1]),
                                    reads=[pb_b, b_cols], writes=[t1_b])
                            dve.op(lambda t1=t1, sgt=sgt, zc=zc, nout=nout: nc.vector.tensor_tensor(
                                out=zT[:, zc, 0:nout], in0=t1[:, 0:nout], in1=sgt[:, 0:nout], op=ALU.mult),
                                reads=[t1_b, sgt_b], writes=[b_zT])
                    if DEBUG_STOP == "p2":
                        break
                    nxt_tiles = p1_tiles(supers[si + 1]) if si + 1 < len(supers) else []
                    for sg in st:
                        ntile = (sg["ohi"] - sg["olo"]) // 128
                        for tt in range(ntile):
                            zc0 = sg["o0"] + tt * 128
                            row = sg["row0"] + sg["olo"] + tt * 128
                            tl = nxt_tiles.pop(0) if nxt_tiles else None
                            stt = p1_A(tl) if tl is not None else None
                            out_proj_tile(lambda kc, zc0=zc0: zT[:, kc, zc0:zc0 + 128], b_zT, wout, b_wout, sg["seg"],
                                          xc[row:row + 128, :], Buf("xin"), xc1[row:row + 128, :], xc1_buf, big, smalls, junk)
                            if tl is not None:
                                p1_B(tl, stt)
                    p1_run(nxt_tiles)
                fw.barrier()

        def layer1():
            with SB("G_rows1", [128, 1, D], F32) as g_:
                G_box[0] = g_
                layer1_body()

        def layer1_body():
            ada_layer(1, False)
            NKT = NTOK // 128
            with ExitStack() as s1:
                E1 = s1.enter_context
                oT = E1(SB("oT", [128, H, L], BF16))
                b_oT = [Buf(f"oT{h}") for h in range(H)]
                small_t = E1(SB("small1", [128, 8, 8], F32))
                smalls = Ring([(small_t[:, j, 0:1], small_t[:, j, 1:2], small_t[:, j, 4:8]) for j in range(8)], "small1")
                with ExitStack() as sab:
                    EAB = sab.enter_context
                    ckvnT = EAB(SB("ckvnT", [128, 4, NTOK], BF16))
                    kr2T = EAB(SB("kr2T", [128, NTOK], BF16))
                    cqnT = EAB(SB("cqnT", [128, 4, L], BF16))
                    b_ckvn, b_kr2, b_cqn = Buf("ckvnT"), Buf("kr2T"), Buf("cqnT")
                    qn_col = EAB(SB("qn_col", [128, 4], F32))
                    kvn_col = EAB(SB("kvn_col", [128, 4], F32))
                    b_ncol = Buf("ncol")
                    load_cols(qn_col[:], b_ncol, mla_q_norm.rearrange("(c p) -> c p", p=128), 4)
                    load_cols(kvn_col[:], b_ncol, mla_kv_norm.rearrange("(c p) -> c p", p=128), 4)
                    with ExitStack() as sa:
                        EA = sa.enter_context
                        junk = EA(SB("junkA", [128, 512], BF16))
                        WA = EA(SB("WA", [128, KC, QL + KVL + DR], BF16))
                        b_WA = Buf("WA")
                        for (c0, c1) in ((QL, QL + KVL + DR), (0, QL)):
                            pool.dma(WA[:, :, c0:c1], mla_w_in[:, c0:c1].rearrange("(kc p) n -> p kc n", p=128), writes=[b_WA])
                        hA_t = EA(SB("hA", [128, 4, KC, 128], BF16))
                        hAr = Ring([hA_t[:, j] for j in range(4)], "hA")
                        big_t = EA(SB("bigA", [128, 3, D], F32))
                        big = Ring([big_t[:, j] for j in range(3)], "bigA")
                        xn_t = EA(SB("xnA", [128, 2, D], BF16))
                        xnr = Ring([xn_t[:, j] for j in range(2)], "xnA")
                        cn_t = EA(SB("cn", [128, 6, 512], BF16))
                        cnr = Ring([cn_t[:, j] for j in range(6)], "cn")
                        kr_t = EA(SB("kr", [128, 2, 64], F32))
                        krr = Ring([kr_t[:, j] for j in range(2)], "kr")
                        kro_t = EA(SB("kro", [128, 4, 128], BF16))
                        kror = Ring([kro_t[:, j] for j in range(4)], "kro")
                        rt_t = EA(SB("rt", [128, 2, 2, 32], F32))
                        rtr = Ring([rt_t[:, j] for j in range(2)], "rt")
                        tt_t = EA(SB("ropetmp", [128, 4, 32], F32))
                        b_tt = Buf("ropetmp")

                        def rms512_A(pb, pb_b, small):
                            (ss, rstd, _), sm_b = small
                            cn, cn_b = cnr.next()
                            act.op(lambda: nc.scalar.activation(out=junk[:, 0:512], in_=pb[:, 0:512], func=AF.Square, accum_out=ss),
                                   reads=[pb_b], writes=[sm_b])
                            act.op(lambda: nc.scalar.activation(out=rstd, in_=ss, func=AF.Sqrt, scale=1.0 / 512, bias=eps_col[:]),
                                   reads=[sm_b, eps_b], writes=[sm_b])
                            dve.op(lambda: nc.vector.reciprocal(out=rstd, in_=rstd), reads=[sm_b], writes=[sm_b])
                            dve.op(lambda: nc.vector.tensor_scalar(out=cn[:], in0=pb[:, 0:512], scalar1=rstd, scalar2=None, op0=ALU.mult),
                                   reads=[pb_b, sm_b], writes=[cn_b])
                            return cn, cn_b

                        def rms512_B(items_):
                            n_it = len(items_)
                            for k_, (cnst, gcol, dst, dst_b) in enumerate(items_):
                                cn, cn_b = cnst
                                for c in range(4):
                                    pe.op(lambda c=c: nc.tensor.transpose(ptr[:, 1, k_ * 512 + c * 128:k_ * 512 + (c + 1) * 128], cn[:, c * 128:(c + 1) * 128], ident[:]),
                                          reads=[cn_b, ident_b], writes=[ptr_bufs[1]], signal=(c == 3 and k_ == n_it - 1))
                            for k_, (cnst, gcol, dst, dst_b) in enumerate(items_):
                                dve.op(lambda: nc.vector.tensor_tensor(out=dst, in0=ptr[:, 1, k_ * 512:(k_ + 1) * 512].rearrange("p (c t) -> p c t", c=4),
                                                                       in1=gcol[:].unsqueeze(2).to_broadcast([128, 4, 128]), op=ALU.mult),
                                       reads=[ptr_bufs[1], b_ncol], writes=[dst_b])

                        st_ = {}

                        def stA_N(ti):
                            row = ti * 128
                            seg = 0 if ti >= LC // 128 else 1
                            hA, hA_b = hAr.next()
                            norm_tile_to_hT(xc1[row:row + 128, :], xc1_buf, 128, seg, big.next(), xnr.next(), smalls.next(),
                                            lambda fc, hA=hA: hA[:, fc, :], hA_b, junk)
                            st_[ti] = dict(hA=hA, hA_b=hA_b)

                        def stA_P(ti):
                            is_lat = ti >= LC // 128
                            hA, hA_b = st_[ti]["hA"], st_[ti]["hA_b"]
                            pkv, pkv_b = pring.next()
                            pkr, pkr_b = pring.next()
                            outs = [(pkv, pkv_b, QL, KVL), (pkr, pkr_b, QL + KVL, DR)]
                            if is_lat:
                                pq, pq_b = pring.next()
                                outs.append((pq, pq_b, 0, QL))
                                st_[ti].update(pq=pq, pq_b=pq_b)
                            for (pb, pb_b, c0, w) in outs:
                                for kc in range(KC):
                                    pe.op(lambda kc=kc, pb=pb, c0=c0, w=w, hA=hA: nc.tensor.matmul(
                                        pb[:, 0:w], lhsT=hA[:, kc, :], rhs=WA[:, kc, c0:c0 + w], start=(kc == 0), stop=(kc == KC - 1)),
                                        reads=[hA_b, b_WA], writes=[pb_b], signal=(kc == KC - 1))
                            st_[ti].update(pkv=pkv, pkv_b=pkv_b, pkr=pkr, pkr_b=pkr_b)

                        rst_ = {}

                        def stA_Ra(ti):
                            row = ti * 128
                            is_lat = ti >= LC // 128
                            S_ = st_.pop(ti)
                            pkv, pkv_b, pkr, pkr_b = S_["pkv"], S_["pkv_b"], S_["pkr"], S_["pkr_b"]
                            R_ = dict(kv=rms512_A(pkv, pkv_b, smalls.next()))
                            if is_lat:
                                t0 = row - LC
                                R_["q"] = rms512_A(S_["pq"], S_["pq_b"], smalls.next())
                            kr, kr_b = krr.next()
                            kro, kro_b = kror.next()
                            dve.op(lambda kr=kr, pkr=pkr: nc.vector.tensor_copy(out=kr, in_=pkr[:, 0:64]), reads=[pkr_b], writes=[kr_b])
                            if is_lat:
                                rt, rt_b = rtr.next()
                                sp.dma(rt, rope_tok_d[t0:t0 + 128], writes=[rt_b])
                                krv = kr.rearrange("p (h j n) -> p h j n", h=2, j=2)
                                x1, x2 = krv[:, :, 0, :], krv[:, :, 1, :]
                                cosv = rt[:, 0, :].rearrange("p (h n) -> p h n", h=2)
                                sinv = rt[:, 1, :].rearrange("p (h n) -> p h n", h=2)
                                tv = [tt_t[:, j, :].rearrange("p (h n) -> p h n", h=2) for j in range(4)]
                                krov = kro[:, 0:64].rearrange("p (h j n) -> p h j n", h=2, j=2)
                                for (o, a, bb) in ((tv[0], x1, cosv), (tv[1], x2, sinv), (tv[2], x1, sinv), (tv[3], x2, cosv)):
                                    dve.op(lambda o=o, a=a, bb=bb: nc.vector.tensor_tensor(out=o, in0=a, in1=bb, op=ALU.mult),
                                           reads=[kr_b, rt_b], writes=[b_tt])
                                dve.op(lambda: nc.vector.tensor_tensor(out=krov[:, :, 0, :], in0=tv[0], in1=tv[1], op=ALU.subtract),
                                       reads=[b_tt], writes=[kro_b])
                                dve.op(lambda: nc.vector.tensor_tensor(out=krov[:, :, 1, :], in0=tv[2], in1=tv[3], op=ALU.add),
                                       reads=[b_tt], writes=[kro_b])
                            else:
                                dve.op(lambda kr=kr, kro=kro: nc.vector.tensor_copy(out=kro[:, 0:64], in_=kr), reads=[kr_b], writes=[kro_b])
                            dve.op(lambda kro=kro: nc.vector.tensor_copy(out=kro[:, 64:128], in_=kro[:, 0:64]), reads=[kro_b], writes=[kro_b])
                            R_["kro"] = (kro, kro_b)
                            rst_[ti] = R_

                        def stA_Rb(ti):
                            row = ti * 128
                            R_ = rst_.pop(ti)
                            its = [(R_["kv"], kvn_col, ckvnT[:, :, row:row + 128], b_ckvn)]
                            if "q" in R_:
                                t0 = row - LC
                                its.append((R_["q"], qn_col, cqnT[:, :, t0:t0 + 128], b_cqn))
                            rms512_B(its)
                            kro, kro_b = R_["kro"]
                            pe.op(lambda kro=kro: nc.tensor.transpose(ptr[:, 0, 0:128], kro, ident[:]), reads=[kro_b, ident_b], writes=[ptr_bufs[0]])
                            act.op(lambda row=row: nc.scalar.copy(out=kr2T[:, row:row + 128], in_=ptr[:, 0, 0:128]), reads=[ptr_bufs[0]], writes=[b_kr2])

                        SKEW = 2
                        for ti in range(min(SKEW, NKT)):
                            stA_N(ti)
                        stA_P(0)
                        for ti in range(NKT):
                            if ti + SKEW < NKT:
                                stA_N(ti + SKEW)
                            if ti + 1 < NKT:
                                stA_P(ti + 1)
                            stA_Ra(ti)
                            if ti >= 1:
                                stA_Rb(ti - 1)
                        stA_Rb(NKT - 1)
                        fw.barrier()
                    if DEBUG_STOP == "A":
                        return
                    with ExitStack() as sb_:
                        EB = sb_.enter_context
                        cs_t = EB(SB("ropecs", [128, L], F32))
                        b_cs = Buf("ropecs")
                        for j in range(2):
                            sp.dma(cs_t[64 * j:64 * j + 64, :], rope_cs_d[j, 0:64, :], writes=[b_cs])
                        wq_t = EB(SB("wq", [128, 2, 4, DN + DR], BF16))
                        wqr_ = Ring([wq_t[:, j] for j in range(2)], "wq")
                        wrot_t = EB(SB("wrot", [128, 2, 4, 2 * DR], BF16))
                        wrotr = Ring([wrot_t[:, j] for j in range(2)], "wrot")
                        wkv_t = EB(SB("wkv", [128, 2, 4, DN + DV], BF16))
                        wkvr = Ring([wkv_t[:, j] for j in range(2)], "wkv")
                        KT_t = EB(SB("KT", [128, 2, NTOK], BF16))
                        KTr = Ring([KT_t[:, j] for j in range(2)], "KT")
                        V_t = EB(SB("V", [128, 2, NKT, DV], BF16))
                        Vr = Ring([V_t[:, j] for j in range(2)], "V")
                        qn_t = EB(SB("qnT", [128, 2, L], BF16))
                        qnr = Ring([qn_t[:, j] for j in range(2)], "qnT")
                        qr_t = EB(SB("qrT", [128, 2, L], BF16))
                        qrr = Ring([qr_t[:, j] for j in range(2)], "qrT")
                        pT_t = EB(SB("pT", [128, 6, 512], BF16))
                        pTr = Ring([pT_t[:, j] for j in range(6)], "pT")
                        sq_t = EB(SB("sq", [128, NTOK], BF16))
                        b_sq = Buf("sq")
                        krsq = EB(SB("krsq", [128, NTOK], BF16))
                        b_krsq = Buf("krsq")
                        qrsq = EB(SB("qrsq", [128, L], BF16))
                        b_qrsq = Buf("qrsq")
                        rden_t = EB(SB("rden", [128, 2, 512], F32))
                        rdenr = Ring([rden_t[:, j] for j in range(2)], "rden")
                        mx_t = EB(SB("mx", [128, 16], F32))
                        b_mx = Buf("mx")
                        nb_t = EB(SB("nbias", [128, 2, 4], F32))
                        nbr = Ring([nb_t[:, j] for j in range(2)], "nbias")
                        pool.op(lambda: nc.gpsimd.memset(krsq[64:128, :], 0.0), writes=[b_krsq])
                        pool.op(lambda: nc.gpsimd.tensor_tensor(out=krsq[0:64, :], in0=kr2T[0:64, :], in1=kr2T[0:64, :], op=ALU.mult),
                                reads=[b_kr2], writes=[b_krsq])
                        kblocks = col_blocks(NTOK)
                        accs = [(pall.slots[4], pall.slots[5]), (pall.slots[6], pall.slots[7])]
                        ring4 = pring4
                        hb = {}
                        sbank = {}

                        def live_banks():
                            return [b for (_, b) in sbank.values()]

                        wts = {}

                        def load_head_weights(h):
                            wq, wq_b = wqr_.next()
                            wrot, wrot_b = wrotr.next()
                            wkv, wkv_b = wkvr.next()
                            pool.dma(wq, mla_w_uq[:, h * (DN + DR):(h + 1) * (DN + DR)].rearrange("(c p) f -> p c f", p=128), writes=[wq_b])
                            pool.dma(wkv, mla_w_ukv[:, h * (DN + DV):(h + 1) * (DN + DV)].rearrange("(c p) f -> p c f", p=128), writes=[wkv_b])
                            wqv = wq[:, :, DN:DN + DR].rearrange("p c (h j n) -> p c h j n", h=2, j=2)
                            wrv = wrot[:, :, DR:2 * DR].rearrange("p c (h j n) -> p c h j n", h=2, j=2)
                            pool.op(lambda: nc.gpsimd.tensor_copy(out=wrot[:, :, 0:DR], in_=wq[:, :, DN:DN + DR]), reads=[wq_b], writes=[wrot_b])
                            for hh in range(2):
                                pool.op(lambda hh=hh: nc.gpsimd.tensor_scalar(out=wrv[:, :, hh, 0, :], in0=wqv[:, :, hh, 1, :], scalar1=-1.0, scalar2=None, op0=ALU.mult),
                                        reads=[wq_b], writes=[wrot_b])
                                pool.op(lambda hh=hh: nc.gpsimd.tensor_copy(out=wrv[:, :, hh, 1, :], in_=wqv[:, :, hh, 0, :]), reads=[wq_b], writes=[wrot_b])
                            wts[h] = (wq, wq_b, wrot, wrot_b, wkv, wkv_b)

                        def prep_gen(h):
                                wq, wq_b, wrot, wrot_b, wkv, wkv_b = wts[h]
                                KT, KT_b = KTr.next()
                                V, V_b = Vr.next()
                                qn, qn_b = qnr.next()
                                qr, qr_b = qrr.next()
                                nb, nb_b = nbr.next()
                                for bi, (s_, sz) in enumerate(kblocks):
                                    pb, pb_b = ring4.next_skip(live_banks())
                                    for c in range(4):
                                        pe.op(lambda c=c, pb=pb, s_=s_, sz=sz: nc.tensor.matmul(
                                            pb[:, 0:sz], lhsT=wkv[:, c, 0:DN], rhs=ckvnT[:, c, s_:s_ + sz], start=(c == 0), stop=(c == 3)),
                                            reads=[wkv_b, b_ckvn], writes=[pb_b], signal=(c == 3))
                                    dve.op(lambda pb=pb, s_=s_, sz=sz: nc.vector.tensor_copy(out=KT[:, s_:s_ + sz], in_=pb[:, 0:sz]), reads=[pb_b], writes=[KT_b])
                                    yield
                                pool.op(lambda: nc.gpsimd.tensor_tensor(out=sq_t[:, 0:NTOK], in0=KT[:, :], in1=KT[:, :], op=ALU.mult), reads=[KT_b], writes=[b_sq])
                                for g4 in range(0, NKT, 4):
                                    nk = min(4, NKT - g4)
                                    pb, pb_b = ring4.next_skip(live_banks())
                                    for j in range(nk):
                                        kt = g4 + j
                                        for c in range(4):
                                            pe.op(lambda c=c, pb=pb, j=j, kt=kt: nc.tensor.matmul(
                                                pb[:, j * 128:(j + 1) * 128], lhsT=ckvnT[:, c, kt * 128:(kt + 1) * 128], rhs=wkv[:, c, DN:DN + DV],
                                                start=(c == 0), stop=(c == 3)),
                                                reads=[wkv_b, b_ckvn], writes=[pb_b], signal=(c == 3))
                                    dve.op(lambda pb=pb, g4=g4, nk=nk: nc.vector.tensor_copy(out=V[:, g4:g4 + nk, :], in_=pb[:, 0:nk * 128].rearrange("p (k d) -> p k d", d=DV)),
                                           reads=[pb_b], writes=[V_b])
                                    yield
                                for _ in range(2):
                                    yield
                                for bi, (s_, sz) in enumerate(kblocks):
                                    pb, pb_b = ring4.next_skip(live_banks())
                                    pe.op(lambda pb=pb, s_=s_, sz=sz: nc.tensor.matmul(pb[:, 0:sz], lhsT=ones_bf[:, :], rhs=sq_t[:, s_:s_ + sz], start=True, stop=False),
                                          reads=[ones_b, b_sq], writes=[pb_b], signal=False)
                                    pe.op(lambda pb=pb, s_=s_, sz=sz: nc.tensor.matmul(pb[:, 0:sz], lhsT=ones_bf[:, :], rhs=krsq[:, s_:s_ + sz], start=False, stop=True),
                                          reads=[ones_b, b_krsq], writes=[pb_b])
                                    dve.op(lambda pb=pb, sz=sz, bi=bi: nc.vector.tensor_reduce(out=mx_t[:, bi:bi + 1], in_=pb[:, 0:sz], axis=mybir.AxisListType.X, op=ALU.max),
                                           reads=[pb_b], writes=[b_mx])
                                    yield
                                for qb in range(4):
                                    q0 = qb * 512
                                    pb, pb_b = ring4.next_skip(live_banks())
                                    for c in range(4):
                                        pe.op(lambda c=c, pb=pb, q0=q0: nc.tensor.matmul(
                                            pb[:, :], lhsT=wq[:, c, 0:DN], rhs=cqnT[:, c, q0:q0 + 512], start=(c == 0), stop=(c == 3)),
                                            reads=[wq_b, b_cqn], writes=[pb_b], signal=(c == 3))
                                    dve.op(lambda pb=pb, q0=q0: nc.vector.tensor_copy(out=qn[:, q0:q0 + 512], in_=pb[:, :]), reads=[pb_b], writes=[qn_b])
                                    yield
                                    p1, p1_b = ring4.next_skip(live_banks())
                                    for c in range(4):
                                        pe.op(lambda c=c, q0=q0: nc.tensor.matmul(
                                            p1[:, :], lhsT=wrot[:, c, :], rhs=cqnT[:, c, q0:q0 + 512], start=(c == 0), stop=(c == 3)),
                                            reads=[wrot_b, b_cqn], writes=[p1_b], signal=(c == 3))
                                    dve.op(lambda q0=q0: nc.vector.tensor_tensor(out=qr[:, q0:q0 + 512], in0=p1[:, :], in1=cs_t[:, q0:q0 + 512], op=ALU.mult),
                                           reads=[p1_b, b_cs], writes=[qr_b])
                                    yield
                                pool.op(lambda: nc.gpsimd.tensor_tensor(out=sq_t[:, 0:L], in0=qn[:, :], in1=qn[:, :], op=ALU.mult), reads=[qn_b], writes=[b_sq])
                                pool.op(lambda: nc.gpsimd.tensor_tensor(out=qrsq[:, :], in0=qr[:, :], in1=qr[:, :], op=ALU.mult),
                                        reads=[qr_b], writes=[b_qrsq])
                                qsq_r = qrsq
                                for _ in range(9):
                                    yield
                                for qb in range(4):
                                    q0 = qb * 512
                                    pb, pb_b = ring4.next_skip(live_banks())
                                    pe.op(lambda pb=pb, q0=q0: nc.tensor.matmul(pb[:, :], lhsT=ones_bf[:, :], rhs=sq_t[:, q0:q0 + 512], start=True, stop=False),
                                          reads=[ones_b, b_sq], writes=[pb_b], signal=False)
                                    pe.op(lambda pb=pb, q0=q0: nc.tensor.matmul(pb[:, :], lhsT=ones_bf[:, :], rhs=qsq_r[:, q0:q0 + 512], start=False, stop=True),
                                          reads=[ones_b, b_qrsq], writes=[pb_b])
                                    dve.op(lambda pb=pb, qb=qb: nc.vector.tensor_reduce(out=mx_t[:, 8 + qb:9 + qb], in_=pb[:, :], axis=mybir.AxisListType.X, op=ALU.max),
                                           reads=[pb_b], writes=[b_mx])
                                    yield
                                nkb = len(kblocks)
                                dve.op(lambda: nc.vector.tensor_reduce(out=nb[:, 0:1], in_=mx_t[:, 0:nkb], axis=mybir.AxisListType.X, op=ALU.max), reads=[b_mx], writes=[nb_b])
                                dve.op(lambda: nc.vector.tensor_reduce(out=nb[:, 1:2], in_=mx_t[:, 8:12], axis=mybir.AxisListType.X, op=ALU.max), reads=[b_mx], writes=[nb_b])
                                dve.op(lambda: nc.vector.tensor_tensor(out=nb[:, 2:3], in0=nb[:, 0:1], in1=nb[:, 1:2], op=ALU.mult), reads=[nb_b], writes=[nb_b])
                                act.op(lambda: nc.scalar.activation(out=nb[:, 2:3], in_=nb[:, 2:3], func=AF.Sqrt), reads=[nb_b], writes=[nb_b])
                                dve.op(lambda: nc.vector.tensor_scalar(out=nb[:, 3:4], in0=nb[:, 2:3], scalar1=-MLA_SCALE, scalar2=None, op0=ALU.mult), reads=[nb_b], writes=[nb_b])

                                if h + 1 < H:
                                    load_head_weights(h + 1)
                                hb[h] = dict(KT=KT, KT_b=KT_b, V=V, V_b=V_b, qn=qn, qn_b=qn_b, qr=qr, qr_b=qr_b, nb=nb, nb_b=nb_b)

                        items = [(h, qb, kt) for h in range(H) for qb in range(4) for kt in range(NKT)]

                        def S_emit(i):
                            h, qb, kt = items[i]
                            B = hb[h]
                            q0 = qb * 512
                            pb, pb_b = ring4.next_skip(live_banks())
                            pe.op(lambda: nc.tensor.matmul(pb[:, :], lhsT=B["KT"][:, kt * 128:(kt + 1) * 128], rhs=B["qn"][:, q0:q0 + 512], start=True, stop=False),
                                  reads=[B["KT_b"], B["qn_b"]], writes=[pb_b], signal=False)
                            pe.op(lambda: nc.tensor.matmul(pb[:, :], lhsT=kr2T[:, kt * 128:(kt + 1) * 128], rhs=B["qr"][:, q0:q0 + 512], start=False, stop=True),
                                  reads=[b_kr2, B["qr_b"]], writes=[pb_b])
                            sbank[i] = (pb, pb_b)

                        AHEAD = 2
                        load_head_weights(0)
                        for _ in prep_gen(0):
                            pass
                        pgen = None
                        for i in range(min(AHEAD, len(items))):
                            S_emit(i)
                        for i, (h, qb, kt) in enumerate(items):
                            if qb == 0 and kt == 4 and h + 1 < H:
                                pgen = prep_gen(h + 1)
                            if pgen is not None:
                                if i + AHEAD < len(items) and items[i + AHEAD][0] != h:
                                    for _ in pgen:
                                        pass
                                    pgen = None
                                elif next(pgen, "done") == "done":
                                    pgen = None
                            if i + AHEAD < len(items):
                                S_emit(i + AHEAD)
                            B = hb[h]
                            q0 = qb * 512
                            (acc_o, acc_o_b), (acc_d, acc_d_b) = accs[(h * 4 + qb) % 2]
                            pb, pb_b = sbank.pop(i)
                            pt, pt_b = pTr.next()
                            act.op(lambda: nc.scalar.activation(out=pt, in_=pb[:, :], func=AF.Exp, scale=MLA_SCALE, bias=B["nb"][:, 3:4]),
                                   reads=[pb_b, B["nb_b"]], writes=[pt_b])
                            pe.op(lambda: nc.tensor.matmul(acc_o[:, :], lhsT=B["V"][:, kt, :], rhs=pt, start=(kt == 0), stop=(kt == NKT - 1)),
                                  reads=[B["V_b"], pt_b], writes=[acc_o_b], signal=False)
                            pe.op(lambda: nc.tensor.matmul(acc_d[:, :], lhsT=ones_bf[:, :], rhs=pt, start=(kt == 0), stop=(kt == NKT - 1)),
                                  reads=[ones_b, pt_b], writes=[acc_d_b])
                            if kt == NKT - 1:
                                rd, rd_b = rdenr.next()
                                dve.op(lambda: nc.vector.reciprocal(out=rd, in_=acc_d[:, :]), reads=[acc_d_b], writes=[rd_b])
                                dve.op(lambda: nc.vector.tensor_tensor(out=oT[:, h, q0:q0 + 512], in0=acc_o[:, :], in1=rd, op=ALU.mult),
                                       reads=[acc_o_b, rd_b], writes=[b_oT[h]])
                        fw.barrier()
                if DEBUG_STOP == "B":
                    return
                with ExitStack() as sc:
                    EC = sc.enter_context
                    junk = None
                    wout = EC(SB("wout1", [128, KC, D], BF16))
                    b_wout = Buf("wout1")
                    wout_todo1 = list(range(4))

                    def load_wout1_panel():
                        if wout_todo1:
                            n = wout_todo1.pop(0)
                            pool.dma(wout[:, :, n * 512:(n + 1) * 512], mla_w_out[:, n * 512:(n + 1) * 512].rearrange("(kc p) n -> p kc n", p=128), writes=[b_wout])
                    CW = 512
                    NSC = L // CW
                    hC_t = EC(SB("hC", [128, 2, KC, CW], BF16))
                    b_hC = [Buf("hC0"), Buf("hC1")]
                    wg_t = EC(SB("wgC", [128, 2, KC, 128], BF16))
                    wgr = Ring([wg_t[:, j] for j in range(2)], "wgC")
                    sg_t = EC(SB("sgC", [128, 2, CW], BF16))
                    sgr = Ring([sg_t[:, j] for j in range(2)], "sgC")
                    big_t = EC(SB("bigC", [128, 3, D], F32))
                    big = Ring([big_t[:, j] for j in range(3)], "bigC")
                    xn_t = EC(SB("xnC", [128, 1, D], BF16))
                    xnr = Ring([xn_t[:, 0]], "xnC")
                    b_oTc = [[Buf(f"oTc{ch}_{i}") for i in range(NSC)] for ch in range(H)]
                    gcol0 = QL + KVL + DR

                    def normC_A(j, tt):
                        row = LC + j * CW + tt * 128
                        return norm_tile_A(xc1[row:row + 128, :], xc1_buf, 128, big.next(), xnr.next(), smalls.next())

                    def normC_B(j, tt, stt):
                        norm_tile_B(stt, 0, lambda fc: hC_t[:, j % 2, fc, tt * 128:(tt + 1) * 128], b_hC[j % 2])

                    def stage_G_pair(i0):
                        for ch in range(H):
                            wgc, wgc_b = wgr.next()
                            pool.dma(wgc, mla_w_in[:, gcol0 + ch * 128: gcol0 + (ch + 1) * 128].rearrange("(kc p) f -> p kc f", p=128), writes=[wgc_b])
                            if ch % 2 == 1:
                                load_wout1_panel()
                            for i in (i0, i0 + 1):
                                t0 = i * CW
                                pb, pb_b = pring.next()
                                for kc in range(KC):
                                    pe.op(lambda kc=kc: nc.tensor.matmul(pb[:, 0:CW], lhsT=wgc[:, kc, :], rhs=hC_t[:, i % 2, kc, :], start=(kc == 0), stop=(kc == KC - 1)),
                                          reads=[wgc_b, b_hC[i % 2]], writes=[pb_b], signal=(kc == KC - 1))
                                sgc, sgc_b = sgr.next()
                                act.op(lambda: nc.scalar.activation(out=sgc, in_=pb[:, 0:CW], func=AF.Silu), reads=[pb_b], writes=[sgc_b])
                                dve.op(lambda: nc.vector.tensor_tensor(out=oT[:, ch, t0:t0 + CW], in0=oT[:, ch, t0:t0 + CW], in1=sgc, op=ALU.mult),
                                       reads=[sgc_b, b_oTc[ch][i]], writes=[b_oTc[ch][i]])

                    def stage_O_pair(i0):
                        nxt = [(j, tt) for j in (i0 + 2, i0 + 3) if j < NSC for tt in range(CW // 128)]
                        for i in (i0, i0 + 1):
                            for tt in range(CW // 128):
                                tk = i * CW + tt * 128
                                row = LC + tk
                                nt = nxt.pop(0) if nxt else None
                                stt = normC_A(*nt) if nt else None
                                out_proj_tile(lambda kc, tk=tk: oT[:, kc, tk:tk + 128], [b_oTc[ch][i] for ch in range(H)], wout, b_wout, 0,
                                              xc1[row:row + 128, :], xc1_buf, outd[tk:tk + 128, :], out_buf, big, smalls, junk)
                                if nt:
                                    normC_B(nt[0], nt[1], stt)

                    first = [(j, tt) for j in (0, 1) for tt in range(CW // 128)]
                    for nt in first:
                        normC_B(nt[0], nt[1], normC_A(*nt))
                    for i0 in range(0, NSC, 2):
                        stage_G_pair(i0)
                        stage_O_pair(i0)
                    fw.barrier()

        if do_l0:
            layer0()
        if do_l1:
            layer1()
        fw.barrier()
    return nc


def _consts():
    inv_edge = np.zeros((4, 16), np.float32)
    for g, w in enumerate(POOL_WINDOWS):
        left = w // 2
        for t in range(8):
            inv_edge[g, t] = 1.0 / min(w, t + w - left)
            tr = 8 - t
            inv_edge[g, 8 + t] = 1.0 / min(w, tr + left)
    n = DR // 4
    freqs = (np.float32(10000.0) ** (-np.arange(n, dtype=np.float32) / np.float32(n))).astype(np.float32)
    t = np.arange(L)
    row = (t // GRID_W).astype(np.float32)
    col = (t % GRID_W).astype(np.float32)
    ang = np.stack([row[:, None] * freqs[None, :], col[:, None] * freqs[None, :]], axis=1).astype(np.float32)
    cos, sin = np.cos(ang).astype(np.float32), np.sin(ang).astype(np.float32)
    rope_tok = np.stack([cos.reshape(L, 2 * n), sin.reshape(L, 2 * n)], axis=1)
    cosT = np.repeat(cos[:, :, None, :], 2, axis=2).reshape(L, DR).T
    sinT = np.repeat(sin[:, :, None, :], 2, axis=2).reshape(L, DR).T
    rope_cs = np.stack([np.concatenate([cosT, cosT], 0), np.concatenate([sinT, sinT], 0)], 0)
    return dict(ident=np.eye(128, dtype=np.float32), inv_edge=inv_edge,
                rope_tok=np.ascontiguousarray(rope_tok, dtype=np.float32), rope_cs=np.ascontiguousarray(rope_cs, dtype=np.float32))


_CACHE = {}


def _prog(stage):
    if stage not in _CACHE:
        _CACHE[stage] = build(stage)
    return _CACHE[stage]


def make_in_maps(inputs, stage, cores=8, xc1=None):
    f = lambda k: np.ascontiguousarray(np.asarray(inputs[k], dtype=np.float32))
    cst = _consts()
    x, c, ctx, c_ctx = f("x"), f("c"), f("ctx"), f("c_ctx")
    shared = dict(ada_w=f("ada_w"), ada_b=f("ada_b"), pre_norm=f("pre_norm"), post_norm=f("post_norm"), ident=cst["ident"])
    if stage in ("l0", "fused"):
        shared.update(pool_w_in=f("pool_w_in")[0], pool_w_grp=f("pool_w_grp")[0], pool_b_grp=f("pool_b_grp")[0],
                      pool_scale=f("pool_scale")[0], pool_w_out=f("pool_w_out")[0], inv_edge=cst["inv_edge"])
    if stage in ("l1", "fused"):
        shared.update(mla_w_in=f("mla_w_in")[0], mla_q_norm=f("mla_q_norm")[0], mla_kv_norm=f("mla_kv_norm")[0],
                      mla_w_uq=f("mla_w_uq")[0], mla_w_ukv=f("mla_w_ukv")[0], mla_w_out=f("mla_w_out")[0],
                      rope_tok=cst["rope_tok"], rope_cs=cst["rope_cs"])
    in_maps = []
    for b in range(cores):
        m = dict(shared)
        if stage in ("l0", "fused"):
            m["xc"] = np.concatenate([ctx[b], x[b]], axis=0)
        if stage == "l1":
            m["xc1"] = xc1[b]
        m["cvec"] = np.stack([c[b], c_ctx], axis=0)
        in_maps.append(m)
    return in_maps


def run_l0(inputs, cores=8):
    res = run_bass_kernel_spmd(_prog("l0"), make_in_maps(inputs, "l0", cores), core_ids=list(range(cores)))
    return [r["xc1"] for r in res.results]


N_CORES = 8
MODE = "fused"


def kernel(**inputs):
    if MODE == "fused":
        res = run_bass_kernel_spmd(_prog("fused"), make_in_maps(inputs, "fused", N_CORES), core_ids=list(range(N_CORES)))
    else:
        r0 = run_bass_kernel_spmd(_prog("l0"), make_in_maps(inputs, "l0", N_CORES), core_ids=list(range(N_CORES)))
        xc1 = [np.asarray(r["xc1"]) for r in r0.results]
        res = run_bass_kernel_spmd(_prog("l1"), make_in_maps(inputs, "l1", N_CORES, xc1=xc1), core_ids=list(range(N_CORES)))
    return np.stack([np.asarray(r["out"], dtype=np.float32) for r in res.results], axis=0)
```
